# Optimizing a Trainium2 kernel written in Bass

```python
import jax, jax.numpy as jnp
from jax import lax
import numpy as np

D_MODEL = 1024
BATCH = 1
SEQ = 16384
DEPTH = 4
DEC_BATCH = 2
DEC_SEQ = 8192
PAST_LEN = 128

N_HEADS = 4
HEAD_DIM = 128
MLSTM_WIDTH = N_HEADS * HEAD_DIM
CONV_WIDTH = D_MODEL - MLSTM_WIDTH
CONV_K = 3
CHUNK = 64
N_DIR = 2
N_GATE_COLS = N_DIR * 2 * N_HEADS
PROJ_WIDTH = 4 * MLSTM_WIDTH + N_GATE_COLS + 3 * CONV_WIDTH
D_FF = 4 * D_MODEL
ALPHA = (2.0 * DEPTH) ** 0.25
BETA = (8.0 * DEPTH) ** -0.25
LN_EPS = 1e-5

kernel_name = "hymba_mlstm_shortconv_deepnorm_encoder"

_SPLITS = list(np.cumsum([MLSTM_WIDTH, MLSTM_WIDTH, MLSTM_WIDTH, MLSTM_WIDTH,
                          N_GATE_COLS, CONV_WIDTH, CONV_WIDTH]))


def layer_norm(x, g, b):
    xf = x.astype(jnp.float32)
    mu = jnp.mean(xf, axis=-1, keepdims=True)
    xc = xf - mu
    var = jnp.mean(xc * xc, axis=-1, keepdims=True)
    y = xc * lax.rsqrt(var + LN_EPS) * g.astype(jnp.float32) + b.astype(jnp.float32)
    return y.astype(x.dtype)


def mlstm_chunkwise(q, k, v, ig, lf):
    n_b, n_h, s_len, dh = q.shape
    nc = s_len // CHUNK

    def to_chunks(t):
        t = t.reshape(n_b, n_h, nc, CHUNK, *t.shape[3:])
        return jnp.moveaxis(t, 2, 0)

    qc, kc, vc, ic = to_chunks(q), to_chunks(k), to_chunks(v), to_chunks(ig)
    bc = jnp.cumsum(to_chunks(lf), axis=-1)
    causal_in_scan = jnp.tril(jnp.ones((CHUNK, CHUNK), dtype=bool))

    def step(carry, inp):
        C, n, m = carry
        qt, kt, vt, it, bt = inp
        g = bt[..., -1]
        D = bt[..., :, None] - bt[..., None, :] + it[..., None, :]
        D = jnp.where(causal_in_scan, D, -jnp.inf)
        inter = bt + m[..., None]
        m_t = jnp.maximum(inter, jnp.max(D, axis=-1))
        wD = jnp.exp(D - m_t[..., None])
        w_inter = jnp.exp(inter - m_t)
        s = jnp.einsum('nhtd,nhsd->nhts', qt, kt) * wD
        num = (jnp.einsum('nhts,nhse->nhte', s, vt)
               + w_inter[..., None] * jnp.einsum('nhed,nhtd->nhte', C, qt))
        den = jnp.sum(s, axis=-1) + w_inter * jnp.einsum('nhd,nhtd->nht', n, qt)
        h = num / jnp.maximum(jnp.abs(den), jnp.exp(-m_t))[..., None]
        a = g[..., None] - bt + it
        m_new = jnp.maximum(g + m, jnp.max(a, axis=-1))
        wa = jnp.exp(a - m_new[..., None])
        wc = jnp.exp(g + m - m_new)
        C_new = wc[..., None, None] * C + jnp.einsum('nhs,nhse,nhsd->nhed', wa, vt, kt)
        n_new = wc[..., None] * n + jnp.einsum('nhs,nhsd->nhd', wa, kt)
        return (C_new, n_new, m_new), h

    init = (jnp.zeros((n_b, n_h, dh, dh), jnp.float32),
            jnp.zeros((n_b, n_h, dh), jnp.float32),
            jnp.zeros((n_b, n_h), jnp.float32))
    _, hs = lax.scan(step, init, (qc, kc, vc, ic, bc))
    return jnp.moveaxis(hs, 0, 2).reshape(n_b, n_h, s_len, dh)


def token_mixer(x, w_in, b_gate, mh_norm_w, conv_w, w_out):
    bsz, s_len, _ = x.shape
    proj = x @ w_in
    q, k, v, o, gates, cb, cc, ch = jnp.split(proj, _SPLITS, axis=-1)

    def heads(t):
        return t.reshape(bsz, s_len, N_HEADS, HEAD_DIM).transpose(0, 2, 1, 3).astype(jnp.float32)

    def both_dirs(t):
        return jnp.concatenate([t, jnp.flip(t, axis=2)], axis=0)

    gt = gates.astype(jnp.float32).reshape(bsz, s_len, N_DIR, 2, N_HEADS) + b_gate.astype(jnp.float32)
    gt = gt.transpose(2, 3, 0, 4, 1)
    ig = jnp.concatenate([gt[0, 0], jnp.flip(gt[1, 0], axis=-1)], axis=0)
    lf = jax.nn.log_sigmoid(jnp.concatenate([gt[0, 1], jnp.flip(gt[1, 1], axis=-1)], axis=0))
    qh = heads(q) * (HEAD_DIM ** -0.5)
    h = mlstm_chunkwise(both_dirs(qh), both_dirs(heads(k)), both_dirs(heads(v)), ig, lf)
    h = h[:bsz] + jnp.flip(h[bsz:], axis=2)
    mu = jnp.mean(h, axis=-1, keepdims=True)
    hc = h - mu
    h = hc * lax.rsqrt(jnp.mean(hc * hc, axis=-1, keepdims=True) + LN_EPS)
    h = h.transpose(0, 2, 1, 3).reshape(bsz, s_len, MLSTM_WIDTH) * mh_norm_w.astype(jnp.float32)
    h_m = (jax.nn.sigmoid(o.astype(jnp.float32)) * h).astype(x.dtype)

    u = cc * ch
    y = lax.conv_general_dilated(
        u, conv_w[:, None, :].astype(u.dtype), window_strides=(1,),
        padding=((CONV_K // 2, CONV_K // 2),),
        dimension_numbers=('NWC', 'WIO', 'NWC'),
        feature_group_count=CONV_WIDTH)
    h_c = cb * y

    return jnp.concatenate([h_m, h_c], axis=-1) @ w_out


def trunk(x, w_in, b_gate, mh_norm_w, conv_w, w_out, ln1_g, ln1_b, w_ff1, w_ff2, ln2_g, ln2_b):
    for l in range(DEPTH):
        mix = token_mixer(x, w_in[l], b_gate[l], mh_norm_w[l], conv_w[l], w_out[l])
        x = layer_norm(ALPHA * x + mix, ln1_g[l], ln1_b[l])
        ff = jnp.square(jax.nn.relu(x @ w_ff1[l])) @ w_ff2[l]
        x = layer_norm(ALPHA * x + ff, ln2_g[l], ln2_b[l])
    return x


def setup_inputs(seed: int = 0) -> dict:
    key = jax.random.key(seed)
    ks = jax.random.split(key, 16)
    f32 = jnp.float32
    x_prompt = jax.random.normal(ks[0], (BATCH, SEQ, D_MODEL), f32)
    x_sample = jax.random.normal(ks[1], (DEC_BATCH, DEC_SEQ, D_MODEL), f32)
    w_in = jax.random.normal(ks[2], (DEPTH, D_MODEL, PROJ_WIDTH), f32) * D_MODEL ** -0.5
    b_i = 0.1 * jax.random.normal(ks[3], (DEPTH, N_DIR, N_HEADS), f32)
    b_f = jnp.linspace(3.0, 6.0, N_HEADS, dtype=f32) + 0.1 * jax.random.normal(ks[4], (DEPTH, N_DIR, N_HEADS), f32)
    b_gate = jnp.stack([b_i, b_f], axis=2)
    mh_norm_w = 1.0 + 0.02 * jax.random.normal(ks[5], (DEPTH, MLSTM_WIDTH), f32)
    conv_w = jax.random.normal(ks[6], (DEPTH, CONV_K, CONV_WIDTH), f32) * CONV_K ** -0.5
    w_out = jax.random.normal(ks[7], (DEPTH, D_MODEL, D_MODEL), f32) * (D_MODEL ** -0.5 * BETA)
    ln1_g = 1.0 + 0.02 * jax.random.normal(ks[8], (DEPTH, D_MODEL), f32)
    ln1_b = 0.02 * jax.random.normal(ks[9], (DEPTH, D_MODEL), f32)
    w_ff1 = jax.random.normal(ks[10], (DEPTH, D_MODEL, D_FF), f32) * D_MODEL ** -0.5
    w_ff2 = jax.random.normal(ks[11], (DEPTH, D_FF, D_MODEL), f32) * (D_FF ** -0.5 * BETA)
    ln2_g = 1.0 + 0.02 * jax.random.normal(ks[12], (DEPTH, D_MODEL), f32)
    ln2_b = 0.02 * jax.random.normal(ks[13], (DEPTH, D_MODEL), f32)
    return {"x_prompt": x_prompt, "x_sample": x_sample, "w_in": w_in, "b_gate": b_gate,
            "mh_norm_w": mh_norm_w, "conv_w": conv_w, "w_out": w_out,
            "ln1_g": ln1_g, "ln1_b": ln1_b, "w_ff1": w_ff1, "w_ff2": w_ff2,
            "ln2_g": ln2_g, "ln2_b": ln2_b}


def reference(x_prompt, x_sample, w_in, b_gate, mh_norm_w, conv_w, w_out,
              ln1_g, ln1_b, w_ff1, w_ff2, ln2_g, ln2_b):
    y_prompt = trunk(x_prompt, w_in, b_gate, mh_norm_w, conv_w, w_out,
                     ln1_g, ln1_b, w_ff1, w_ff2, ln2_g, ln2_b)
    y_sample = trunk(x_sample, w_in, b_gate, mh_norm_w, conv_w, w_out,
                     ln1_g, ln1_b, w_ff1, w_ff2, ln2_g, ln2_b)
    return (y_prompt, y_sample)
```

```python
import numpy as np
from contextlib import ExitStack
import concourse.bass as bass
import concourse.mybir as mybir
from concourse.bass_utils import run_bass_kernel_spmd

F32 = mybir.dt.float32
BF16 = mybir.dt.bfloat16
AF = mybir.ActivationFunctionType
ALU = mybir.AluOpType

D = 1024
DEPTH = 4
NH = 4
DH = 128
PROJ = 3600
DFF = 4096
ALPHA = (2.0 * DEPTH) ** 0.25
EPS = 1e-5
G = 512
TPG = 4


class Buf:
    __slots__ = ("name", "w", "r")

    def __init__(self, name):
        self.name = name
        self.w = None
        self.r = []


class Sched:
    def __init__(self, nc, es, n_dma_sems=24):
        self.nc = nc
        self.eng = {"pe": nc.tensor, "act": nc.scalar, "dve": nc.vector, "pool": nc.gpsimd, "sp": nc.sync}
        self.sem = {e: es.enter_context(nc.semaphore("c_" + e)) for e in ("pe", "act", "dve", "pool")}
        self.cnt = {e: 0 for e in self.sem}
        self.seen = {e: {} for e in self.eng}
        self.dsem = {}
        for q, n in (("sp", n_dma_sems), ("pool", 48)):
            self.dsem[q] = [[es.enter_context(nc.semaphore("d_%s%d" % (q, i))), 0] for i in range(n)]
        self.dnext = {"sp": 0, "pool": 0}
        self.n_inst = 0
        self.n_wait = 0

    def _wait(self, e, tok):
        if tok is None:
            return
        sem, val, key = tok
        if key == e and e == "pe":
            return
        if self.seen[e].get(key, 0) >= val:
            return
        self.eng[e].wait_ge(sem, val)
        self.seen[e][key] = val
        self.n_wait += 1

    def _deps(self, e, reads, writes):
        for b in reads:
            self._wait(e, b.w)
        for b in writes:
            self._wait(e, b.w)
            for t in b.r:
                self._wait(e, t)

    def _commit(self, tok, reads, writes):
        for b in reads:
            b.r.append(tok)
            if len(b.r) > 6:
                d = {}
                for t in b.r:
                    if t[2] not in d or d[t[2]][1] < t[1]:
                        d[t[2]] = t
                b.r = list(d.values())
        for b in writes:
            b.w = tok
            b.r = []

    def _flat(self, bs):
        out = []
        for b in bs:
            if isinstance(b, (list, tuple)):
                out.extend(self._flat(b))
            else:
                out.append(b)
        return out

    def op(self, e, fn, reads=(), writes=(), inc=True):
        reads = self._flat(reads); writes = self._flat(writes)
        self._deps(e, reads, writes)
        inst = fn(self.eng[e])
        self.n_inst += 1
        if inc:
            self.cnt[e] += 1
            inst.then_inc(self.sem[e], 1)
            tok = (self.sem[e], self.cnt[e], e)
        else:
            tok = (self.sem[e], self.cnt[e] + 1, e)
        self._commit(tok, reads, writes)
        return tok

    def dma(self, q, fn, reads=(), writes=()):
        slot = self.dnext[q]
        self.dnext[q] = (slot + 1) % len(self.dsem[q])
        ent = self.dsem[q][slot]
        key = "%s%d" % (q, slot)
        reads = self._flat(reads); writes = self._flat(writes)
        if ent[1] > 0:
            self._wait(q, (ent[0], ent[1], key))
        self._deps(q, reads, writes)
        inst = fn(self.eng[q])
        self.n_inst += 1
        ent[1] += 16
        inst.then_inc(ent[0], 16)
        tok = (ent[0], ent[1], key)
        self._commit(tok, reads, writes)
        return tok

    def barrier(self, waiter, on):
        for i, ent in enumerate(self.dsem[on]):
            if ent[1] > 0:
                self._wait(waiter, (ent[0], ent[1], "%s%d" % (on, i)))

    def finish(self):
        for q in ("sp", "pool"):
            for i, ent in enumerate(self.dsem[q]):
                if ent[1] > 0:
                    self._wait(q, (ent[0], ent[1], "%s%d" % (q, i)))


def build_program(T, depth):
    NCH = T // 128
    NG = T // G
    MIDC = NCH // 2
    MIDG = NG // 2
    nc = bass.Bass("TRN2", target_bir_lowering=False)
    x_in = nc.dram_tensor("x", [T, D], F32, kind="ExternalInput").ap()
    brk_in = nc.dram_tensor("brk", [128, 1], F32, kind="ExternalInput").ap()
    hmask_in = nc.dram_tensor("hmask", [2, NG], F32, kind="ExternalInput").ap()
    w_in = nc.dram_tensor("w_in", [depth, D, PROJ], F32, kind="ExternalInput").ap()
    b_gate = nc.dram_tensor("b_gate", [depth, 16], F32, kind="ExternalInput").ap()
    mhw_in = nc.dram_tensor("mh_norm_w", [depth, 512], F32, kind="ExternalInput").ap()
    convw_in = nc.dram_tensor("conv_w", [depth, 3, 512], F32, kind="ExternalInput").ap()
    w_out = nc.dram_tensor("w_out", [depth, D, D], F32, kind="ExternalInput").ap()
    ln1g = nc.dram_tensor("ln1_g", [depth, D], F32, kind="ExternalInput").ap()
    ln1b = nc.dram_tensor("ln1_b", [depth, D], F32, kind="ExternalInput").ap()
    w_ff1 = nc.dram_tensor("w_ff1", [depth, D, DFF], F32, kind="ExternalInput").ap()
    w_ff2 = nc.dram_tensor("w_ff2", [depth, DFF, D], F32, kind="ExternalInput").ap()
    ln2g = nc.dram_tensor("ln2_g", [depth, D], F32, kind="ExternalInput").ap()
    ln2b = nc.dram_tensor("ln2_b", [depth, D], F32, kind="ExternalInput").ap()
    y_out = nc.dram_tensor("y", [T, D], F32, kind="ExternalOutput").ap()

    def dscr(name, shape, dt):
        return nc.dram_tensor(name, shape, dt, kind="Internal").ap()

    xs = [dscr("xs0", [T, D], F32), dscr("xs1", [T, D], F32)]
    SBd = dscr("sbst", [NCH, 128, NH * 129], BF16)
    WA = [dscr("WA%d" % l, [128, 8, 1040], BF16) for l in range(depth)]
    WH = [[dscr("WH%d_%d" % (l, h), [128, 8, 512], BF16) for h in range(NH)] for l in range(depth)]
    WG = [dscr("WG%d" % l, [128, 8, 16], BF16) for l in range(depth)]
    WC = [[dscr("WC%d_%d" % (l, j), [128, 8, 384], BF16) for j in range(4)] for l in range(depth)]
    WO = [[dscr("WO%d_%d" % (l, j), [128, 8, 512], BF16) for j in range(2)] for l in range(depth)]
    W1 = [[dscr("W1%d_%d" % (l, j), [128, 8, 512], BF16) for j in range(8)] for l in range(depth)]
    W2 = [[dscr("W2%d_%d" % (l, j), [128, 4, 1024], BF16) for j in range(8)] for l in range(depth)]

    bXl = [Buf("xl")]
    bXn = [Buf("xn")]
    bSB = [Buf("sbd")]
    with ExitStack() as es:
        S = Sched(nc, es)

        def sb(name, shape, dt):
            return es.enter_context(nc.sbuf_tensor("s_" + name, shape, dt))

        def ps(name, shape, dt):
            return es.enter_context(nc.psum_tensor("p_" + name, shape, dt))

        ident = sb("ident", [128, 128], BF16)
        UT = sb("UT", [128, 128], F32)
        LT = sb("LT", [128, 128], F32)
        NSG = sb("NSG", [128, 128], F32)
        NSL = sb("NSL", [128, 128], F32)
        ONES = sb("ONES", [128, 128], F32)
        mask2 = sb("mask2", [128, 2, 128], F32)
        brk = sb("brk", [128, 1], F32)
        hmask = sb("hmask", [2, NG], F32)
        bC = Buf("const")

        def cst(fn):
            S.op("pool", fn, writes=[bC])

        cst(lambda e: e.memset(ident[:], 0.0))
        cst(lambda e: e.affine_select(out=ident[:], in_=ident[:], pattern=[[-1, 128]], compare_op=ALU.not_equal, fill=1.0, base=0, channel_multiplier=1))
        cst(lambda e: e.memset(ONES[:], 1.0))
        cst(lambda e: e.memset(UT[:], 1.0))
        cst(lambda e: e.affine_select(out=UT[:], in_=UT[:], pattern=[[1, 128]], compare_op=ALU.is_ge, fill=0.0, base=0, channel_multiplier=-1))
        cst(lambda e: e.memset(LT[:], 1.0))
        cst(lambda e: e.affine_select(out=LT[:], in_=LT[:], pattern=[[-1, 128]], compare_op=ALU.is_ge, fill=0.0, base=0, channel_multiplier=1))
        cst(lambda e: e.memset(NSG[:], -1.0))
        cst(lambda e: e.affine_select(out=NSG[:], in_=NSG[:], pattern=[[-1, 128]], compare_op=ALU.is_gt, fill=0.0, base=0, channel_multiplier=1))
        cst(lambda e: e.memset(NSL[:], -1.0))
        cst(lambda e: e.affine_select(out=NSL[:], in_=NSL[:], pattern=[[1, 128]], compare_op=ALU.is_gt, fill=0.0, base=0, channel_multiplier=-1))
        cst(lambda e: e.tensor_copy(out=mask2[:, 0, :], in_=UT[:]))
        cst(lambda e: e.tensor_copy(out=mask2[:, 1, :], in_=LT[:]))
        S.dma("sp", lambda e: e.dma_start(out=brk[:], in_=brk_in), writes=[bC])
        S.dma("sp", lambda e: e.dma_start(out=hmask[:], in_=hmask_in), writes=[bC])

        bW = [Buf("wconv%d" % l) for l in range(depth)]

        def conv_dma(l, dst, src):
            S.dma("pool", lambda e: e.dma_start(out=dst, in_=src))

        def wsrc(w2d, c0, n):
            return w2d[:, c0:c0 + n].rearrange("(k p) n -> p k n", p=128)

        def convert_A(l):
            wi = w_in[l]
            conv_dma(l, WA[l][:, :, 0:512], wsrc(wi, 512, 512))
            conv_dma(l, WA[l][:, :, 512:1024], wsrc(wi, 1024, 512))
            conv_dma(l, WA[l][:, :, 1024:1040], wsrc(wi, 2048, 16))
            conv_dma(l, WG[l][:, :, :], wsrc(wi, 2048, 16))

        def convert_layer(l):
            wi = w_in[l]
            for h in range(NH):
                for i, base in enumerate((0, 512, 1024, 1536)):
                    conv_dma(l, WH[l][h][:, :, i * 128:(i + 1) * 128], wsrc(wi, base + h * 128, 128))
            for j in range(4):
                for i, base in enumerate((2064, 2576, 3088)):
                    conv_dma(l, WC[l][j][:, :, i * 128:(i + 1) * 128], wsrc(wi, base + j * 128, 128))
            for j in range(2):
                conv_dma(l, WO[l][j][:, :, :], wsrc(w_out[l], j * 512, 512))
            for j in range(8):
                conv_dma(l, W1[l][j][:, :, :], wsrc(w_ff1[l], j * 512, 512))
            for j in range(8):
                conv_dma(l, W2[l][j][:, :, :], w_ff2[l][j * 512:(j + 1) * 512, :].rearrange("(k p) n -> p k n", p=128))

        convert_A(0)

        NWB = 4
        wbuf = [sb("wb%d" % i, [128, 8, 512], BF16) for i in range(NWB)]
        wbufB = [Buf("wb%d" % i) for i in range(NWB)]
        wnext = [0]
        wg = sb("wg", [128, 8, 16], BF16); bwg = Buf("wg")
        bias16 = sb("bias16", [128, 16], F32)
        mhw = sb("mhw", [128, NH], F32)
        cw = sb("cw", [128, 4, 3], F32)
        g1 = sb("g1", [128, D], F32); b1 = sb("b1", [128, D], F32)
        g2 = sb("g2", [128, D], F32); b2 = sb("b2", [128, D], F32)
        bPar = Buf("params")
        xg = sb("xg", [128, TPG, D], F32); bxg = [Buf("xg%d" % t) for t in range(TPG)]
        xb16 = sb("xb16", [128, D], BF16); bxb16 = Buf("xb16")
        xT = sb("xT", [128, 8, G], BF16); bxT = [Buf("xT%d" % t) for t in range(TPG)]
        xh = sb("xh", [2, D], F32); xh16 = sb("xh16", [2, D], BF16); bxh = Buf("xh"); bxh16 = Buf("xh16")
        xTh = sb("xTh", [128, 8, 2], BF16); bxTh = Buf("xTh")
        hmT = sb("hmT", [128, 8, G], BF16); bhmT = [[Buf("hmT%d_%d" % (k, t)) for t in range(TPG)] for k in range(8)]
        x1T = sb("x1T", [128, 8, G], BF16); bx1T = [Buf("x1T%d" % t) for t in range(TPG)]
        h1T = sb("h1T", [128, 32, G], BF16); bh1T = [Buf("h1T%d" % f) for f in range(32)]
        wa = h1T[:].rearrange("p f t -> p (f t)")[:, 0:8 * 1040].rearrange("p (k n) -> p k n", k=8)
        bwa = [Buf("wa")] + bh1T[0:17]
        relu_t = [sb("relu%d" % i, [128, G], F32) for i in range(2)]; brelu = [Buf("relu%d" % i) for i in range(2)]
        zg = sb("zg", [128, TPG, 16], F32); bzg = Buf("zg")
        nlf = sb("nlf", [128, TPG, 8], F32); bnlf = Buf("nlf")
        eq = sb("eq", [128, TPG, 8], F32); ek = sb("ek", [128, TPG, 8], F32)
        ekk = sb("ekk", [128, TPG, 8], F32); eG = sb("eG", [128, TPG, 8], F32)
        tmp8 = sb("tmp8", [128, TPG, 8], F32); btmp8 = Buf("tmp8")
        bsc = Buf("scalars")
        NQ = 5
        NQB = 9
        qk = [sb("qk%d" % i, [128, 4, 128], BF16) for i in range(NQ)]; bqk = [Buf("qk%d" % i) for i in range(NQ)]
        k2 = [sb("k2%d" % i, [128, 128], BF16) for i in range(NQ)]; bk2 = [Buf("k2%d" % i) for i in range(NQ)]
        vx = [sb("vx%d" % i, [128, 129], BF16) for i in range(NQ)]; bvx = [Buf("vx%d" % i) for i in range(NQ)]
        qkT = [sb("qkT%d" % i, [128, 4, 128], BF16) for i in range(NQ)]; bqkT = [Buf("qkT%d" % i) for i in range(NQ)]
        st16 = [sb("st16%d" % i, [128, 2, 128], BF16) for i in range(NQ)]; bst16 = [Buf("st16%d" % i) for i in range(NQ)]
        sg = [sb("sg%d" % i, [128, 128], F32) for i in range(NQB)]; bsg = [Buf("sg%d" % i) for i in range(NQB)]
        hs = [sb("hs%d" % i, [128, 128], F32) for i in range(NQB)]; bhs = [Buf("hs%d" % i) for i in range(NQB)]
        gm = [sb("gm%d" % i, [128, 128], BF16) for i in range(NQB)]; bgm = [Buf("gm%d" % i) for i in range(NQB)]
        sm = [sb("sm%d" % i, [128, 16], F32) for i in range(NQB)]; bsm = [Buf("sm%d" % i) for i in range(NQB)]
        smL = [sb("smL%d" % i, [128, 4], F32) for i in range(TPG)]; bsmL = [Buf("smL%d" % i) for i in range(TPG)]
        CTf = sb("CTf", [128, NH, 129], F32); bCTf = [Buf("CTf%d" % h) for h in range(NH)]
        CTf16 = sb("CTf16", [128, NH, 129], BF16); bCTf16 = [Buf("CTf16_%d" % h) for h in range(NH)]
        CTb = sb("CTb", [128, NH, 129], F32); bCTb = Buf("CTb")
        CTb16 = [sb("CTb16_%d" % i, [128, NH * 129], BF16) for i in range(2)]; bCTb16 = [Buf("CTb16_%d" % i) for i in range(2)]
        sbg = sb("sbg", [128, TPG, NH * 129], BF16); bsbg = Buf("sbg")
        xa = [sb("xa%d" % i, [128, D], F32) for i in range(3)]; bxa = [Buf("xa%d" % i) for i in range(3)]
        xb16a = [sb("xb16a%d" % i, [128, D], BF16) for i in range(2)]; bxb16a = [Buf("xb16a%d" % i) for i in range(2)]
        xaT = [sb("xaT%d" % i, [128, 8, 128], BF16) for i in range(2)]; bxaT = [Buf("xaT%d" % i) for i in range(2)]
        k2a = [sb("k2a%d" % i, [128, 512], BF16) for i in range(2)]; bk2a = [Buf("k2a%d" % i) for i in range(2)]
        vxa = [sb("vxa%d" % i, [128, NH, 129], BF16) for i in range(2)]; bvxa = [Buf("vxa%d" % i) for i in range(2)]
        zga = sb("zga", [128, 2, 16], F32); bzga = [Buf("zga%d" % i) for i in range(2)]
        sca = sb("sca", [128, 2, 16], F32); bsca = [Buf("sca%d" % i) for i in range(2)]
        ccs = sb("ccs", [128, G + 2], F32); bccs = Buf("ccs")
        uu = sb("uu", [128, G + 2], F32); buu = Buf("uu")
        yy = sb("yy", [128, G], F32); byy = Buf("yy")
        hh = sb("hh", [128, 4], F32); bhh = Buf("hh")
        st6 = sb("st6", [128, 2, 6], F32); bst6 = Buf("st6")

        bank = [ps("bank%d" % i, [128, 512], F32) for i in range(8)]
        bbank = [[Buf("bank%d" % i)] for i in range(8)]
        Rtq = [bbank[3][0]] * 2
        Rst = [bbank[4][0]] * 2
        Rth = [bbank[7][0]] * 2
        R6g = [bbank[6][0]] * 2
        R6u = [bbank[6][0]] * 2

        def bank_bf16(i):
            return bank[i][:].bitcast(BF16)

        for i in range(NQ):
            S.op("pool", lambda e, i=i: e.memset(vx[i][:, 128:129], 1.0), writes=[bvx[i]])
        for i in range(2):
            S.op("pool", lambda e, i=i: e.memset(vxa[i][:, :, 128:129], 1.0), writes=[bvxa[i]])
        print("sbuf bytes remaining", nc.sbuf_bytes_remaining)

        def load_piece(l, src_ap, ncols_total=None, shape3=None):
            i = wnext[0]
            wnext[0] = (i + 1) % NWB
            dst = wbuf[i]
            if shape3 is None:
                S.dma("sp", lambda e: e.dma_start(out=dst[:, :, 0:src_ap.shape[2]], in_=src_ap), writes=[wbufB[i]])
                return dst, wbufB[i]
            v = dst[:].rearrange("p a b -> p (a b)").rearrange("p (a b) -> p a b", a=4)
            S.dma("sp", lambda e: e.dma_start(out=v, in_=src_ap), writes=[wbufB[i]])
            return v, wbufB[i]

        def load_params(l):
            S.dma("sp", lambda e: e.dma_start(out=bias16[:], in_=b_gate[l].partition_broadcast(128)), writes=[bPar])
            S.dma("sp", lambda e: e.dma_start(out=mhw[:], in_=mhw_in[l].rearrange("(h p) -> p h", p=128), allow_slow_non_contiguous=True), writes=[bPar])
            for j in range(4):
                S.dma("sp", lambda e, j=j: e.dma_start(out=cw[:, j, :], in_=convw_in[l][:, j * 128:(j + 1) * 128].rearrange("k p -> p k"), allow_slow_non_contiguous=True), writes=[bPar])
            for dst, src in ((g1, ln1g), (b1, ln1b), (g2, ln2g), (b2, ln2b)):
                S.dma("sp", lambda e, dst=dst, src=src: e.dma_start(out=dst[:], in_=src[l].partition_broadcast(128)), writes=[bPar])

        def gate_scalars(ntile, pgate, bpg, both_dirs):
            nt = ntile
            S.op("dve", lambda e: e.tensor_tensor(out=zg[:, 0:nt, :], in0=pgate, in1=bias16[:].unsqueeze(1).to_broadcast([128, nt, 16]), op=ALU.add),
                 reads=[bpg, bPar], writes=[bzg])
            zf = zg[:, 0:nt, :].rearrange("p t (d g h) -> p t d g h", d=2, g=2)[:, :, :, 1, :]
            S.op("act", lambda e: e.activation(out=nlf[:, 0:nt, :].rearrange("p t (d h) -> p t d h", d=2), in_=zf, func=AF.Exp, scale=-1.0), reads=[bzg], writes=[bnlf])
            S.op("act", lambda e: e.activation(out=nlf[:, 0:nt, :], in_=nlf[:, 0:nt, :], func=AF.Ln, bias=1.0, scale=1.0), reads=[bnlf], writes=[bnlf])

        def ln_rows(zap, bz, gt, bt_, smt, bsmt):
            S.op("dve", lambda e: e.bn_stats(out=st6[:, 0, :], in_=zap[:, 0:512]), reads=[bz], writes=[bst6])
            S.op("dve", lambda e: e.bn_stats(out=st6[:, 1, :], in_=zap[:, 512:1024]), reads=[bz], writes=[bst6])
            S.op("dve", lambda e: e.bn_aggr(out=smt[:, 0:2], in_=st6[:].rearrange("p a b -> p (a b)")), reads=[bst6], writes=[bsmt])
            S.op("dve", lambda e: e.tensor_scalar(out=smt[:, 2:3], in0=smt[:, 1:2], scalar1=EPS, scalar2=None, op0=ALU.add), reads=[bsmt], writes=[bsmt])
            S.op("act", lambda e: e.activation(out=smt[:, 2:3], in_=smt[:, 2:3], func=AF.Ln), reads=[bsmt], writes=[bsmt])
            S.op("act", lambda e: e.activation(out=smt[:, 3:4], in_=smt[:, 2:3], func=AF.Exp, scale=-0.5), reads=[bsmt], writes=[bsmt])
            S.op("dve", lambda e: e.scalar_tensor_tensor(out=zap, in0=zap, scalar=smt[:, 0:1], in1=gt[:], op0=ALU.subtract, op1=ALU.mult), reads=[bz, bsmt, bPar], writes=[bz])
            S.op("dve", lambda e: e.scalar_tensor_tensor(out=zap, in0=zap, scalar=smt[:, 3:4], in1=bt_[:], op0=ALU.mult, op1=ALU.add), reads=[bz, bsmt, bPar], writes=[bz])

        LNQ = float(np.log(DH ** -0.5))

        for l in range(depth):
            xin = x_in if l == 0 else xs[(l - 1) % 2]
            xout = y_out if l == depth - 1 else xs[l % 2]
            S.barrier("sp", "pool")
            if l == 0:
                convert_layer(0)
            if l + 1 < depth:
                convert_A(l + 1)
                convert_layer(l + 1)
            load_params(l)
            S.dma("sp", lambda e: e.dma_start(out=wa, in_=WA[l]), writes=[bwa])
            S.dma("sp", lambda e: e.dma_start(out=wg[:], in_=WG[l]), writes=[bwg])

            S.op("pool", lambda e: e.memset(CTb[:], 0.0), writes=[bCTb])
            order = list(reversed(range(NCH)))

            def A1(ci):
                c = order[ci]; p_ = ci % 2
                xa_t = xa[ci % 3]; bxa_t = bxa[ci % 3]
                S.dma("sp", lambda e: e.dma_start(out=xa_t[:], in_=xin[c * 128:(c + 1) * 128, :]), writes=[bxa_t])
                S.op("act", lambda e: e.copy(out=xb16a[p_][:], in_=xa_t[:]), reads=[bxa_t], writes=[bxb16a[p_]])
                pb = bank_bf16(p_)
                for k in range(8):
                    S.op("pe", lambda e, k=k: e.transpose(out=pb[:, k * 128:(k + 1) * 128], in_=xb16a[p_][:, k * 128:(k + 1) * 128], identity=ident[:]),
                         reads=[bxb16a[p_], bC], writes=[bbank[p_]], inc=(k == 7))
                S.op("dve", lambda e: e.tensor_copy(out=xaT[p_][:].rearrange("p k t -> p (k t)"), in_=pb), reads=[bbank[p_]], writes=[bxaT[p_]])

            def A2(ci):
                p_ = ci % 2
                for (dst, bdst, c0, n) in ((bank[2 + p_][:, :], bbank[2 + p_], 0, 512), (bank[4 + p_][:, :], bbank[4 + p_], 512, 512), (bank[6][:, p_ * 16:(p_ + 1) * 16], R6g[p_], 1024, 16)):
                    for k in range(8):
                        S.op("pe", lambda e, k=k, dst=dst, c0=c0, n=n: e.matmul(dst, lhsT=xaT[p_][:, k, :], rhs=wa[:, k, c0:c0 + n], start=(k == 0), stop=(k == 7)),
                             reads=[bxaT[p_], bwa], writes=[bdst], inc=(k == 7))

            def A3pre(ci):
                p_ = ci % 2
                sc_ = sca[:, p_, :]
                S.op("dve", lambda e: e.tensor_tensor(out=zga[:, p_, :], in0=bank[6][:, p_ * 16:(p_ + 1) * 16], in1=bias16[:], op=ALU.add), reads=[R6g[p_], bPar], writes=[bzga[p_]])
                S.op("act", lambda e: e.activation(out=sc_[:, 0:4], in_=zga[:, p_, 12:16], func=AF.Exp, scale=-1.0), reads=[bzga[p_]], writes=[bsca[p_]])
                S.op("act", lambda e: e.activation(out=sc_[:, 0:4], in_=sc_[:, 0:4], func=AF.Ln, bias=1.0, scale=1.0), reads=[bsca[p_]], writes=[bsca[p_]])

            def A3a(ci):
                c = order[ci]; p_ = ci % 2
                sc_ = sca[:, p_, :]
                o6 = 32 + p_ * 8
                S.op("pe", lambda e: e.matmul(bank[6][:, o6:o6 + 4], lhsT=NSL[:], rhs=sc_[:, 0:4], start=True, stop=True), reads=[bsca[p_], bC], writes=[R6g[p_]], inc=False)
                S.op("pe", lambda e: e.matmul(bank[6][:, o6 + 4:o6 + 8], lhsT=ONES[:], rhs=sc_[:, 0:4], start=True, stop=True), reads=[bsca[p_], bC], writes=[R6g[p_]])
                S.op("dve", lambda e: e.tensor_tensor(out=sc_[:, 4:8], in0=bank[6][:, o6:o6 + 4], in1=zga[:, p_, 8:12], op=ALU.add), reads=[R6g[p_], bzga[p_], bsca[p_]], writes=[bsca[p_]])
                S.op("act", lambda e: e.activation(out=sc_[:, 8:12], in_=sc_[:, 4:8], func=AF.Exp), reads=[bsca[p_]], writes=[bsca[p_]])
                S.op("act", lambda e: e.activation(out=sc_[:, 12:16], in_=bank[6][:, o6 + 4:o6 + 8], func=AF.Exp, scale=-1.0), reads=[R6g[p_], bsca[p_]], writes=[bsca[p_]])
                for h in range(NH):
                    S.op("act", lambda e, h=h: e.activation(out=k2a[p_][:, h * 128:(h + 1) * 128], in_=bank[2 + p_][:, h * 128:(h + 1) * 128], func=AF.Copy, scale=sc_[:, 8 + h:9 + h]),
                         reads=[bbank[2 + p_], bsca[p_]], writes=[bk2a[p_]])
                S.op("dve", lambda e: e.tensor_copy(out=vxa[p_][:, :, 0:128], in_=bank[4 + p_][:].rearrange("p (h e) -> p h e", h=NH)), reads=[bbank[4 + p_]], writes=[bvxa[p_]])

            def A3b(ci):
                c = order[ci]; p_ = ci % 2
                sc_ = sca[:, p_, :]
                u3 = 64 + p_ * 129
                for h in range(NH):
                    dst, bdst = (bank[7][:, h * 129:(h + 1) * 129], bbank[7]) if h < 3 else (bank[6][:, u3:u3 + 129], R6u[p_])
                    S.op("pe", lambda e, h=h, dst=dst: e.matmul(dst, lhsT=k2a[p_][:, h * 128:(h + 1) * 128], rhs=vxa[p_][:, h, :], start=True, stop=True),
                         reads=[bk2a[p_], bvxa[p_]], writes=[bdst], inc=(h in (2, 3)))
                cb16 = CTb16[p_]; bcb16 = bCTb16[p_]
                S.op("pool", lambda e: e.tensor_copy(out=cb16[:], in_=CTb[:].rearrange("p h e -> p (h e)")), reads=[bCTb], writes=[bcb16])
                S.dma("pool", lambda e: e.dma_start(out=SBd[c], in_=cb16[:]), reads=[bcb16])
                for h in range(NH):
                    src, bsrc = (bank[7][:, h * 129:(h + 1) * 129], bbank[7]) if h < 3 else (bank[6][:, u3:u3 + 129], R6u[p_])
                    S.op("dve", lambda e, h=h, src=src: e.scalar_tensor_tensor(out=CTb[:, h, :], in0=CTb[:, h, :], scalar=sc_[:, 12 + h:13 + h], in1=src, op0=ALU.mult, op1=ALU.add),
                         reads=[bCTb, bsca[p_], bsrc], writes=[bCTb])
                if c == MIDC:
                    S.op("dve", lambda e: e.tensor_scalar(out=CTb[:], in0=CTb[:], scalar1=brk[:, 0:1], scalar2=None, op0=ALU.mult), reads=[bCTb, bC], writes=[bCTb])

            for s_ in range(NCH + 3):
                if 0 <= s_ - 2 < NCH:
                    A3a(s_ - 2)
                if 0 <= s_ - 3 < NCH:
                    A3b(s_ - 3)
                if s_ < NCH:
                    A1(s_)
                if 0 <= s_ - 1 < NCH:
                    A2(s_ - 1)
                    A3pre(s_ - 1)

            S.barrier("sp", "pool")
            S.op("pool", lambda e: e.memset(CTf[:], 0.0), writes=bCTf)
            S.op("pool", lambda e: e.memset(CTf16[:], 0.0), writes=bCTf16)

            def prologue(gn):
                t0n = gn * G
                for t in range(TPG):
                    xa_t = xa[t % 3]; bxa_t = bxa[t % 3]
                    S.dma("sp", lambda e, t=t, xa_t=xa_t: e.dma_start(out=xa_t[:], in_=xin[t0n + t * 128:t0n + (t + 1) * 128, :]), writes=[bxa_t])
                    xb_ = xb16a[t % 2]; bxb_ = bxb16a[t % 2]
                    S.op("act", lambda e, xa_t=xa_t, xb_=xb_: e.copy(out=xb_[:], in_=xa_t[:]), reads=[bxa_t], writes=[bxb_])
                    bi = 4 + (t % 2)
                    pb = bank_bf16(bi)
                    for k in range(8):
                        S.op("pe", lambda e, k=k, pb=pb, xb_=xb_: e.transpose(out=pb[:, k * 128:(k + 1) * 128], in_=xb_[:, k * 128:(k + 1) * 128], identity=ident[:]),
                             reads=[bxb_, bC], writes=[bbank[bi]], inc=(k == 7))
                    S.op("dve", lambda e, t=t, pb=pb: e.tensor_copy(out=xT[:, :, t * 128:(t + 1) * 128], in_=pb.rearrange("p (k t) -> p k t", k=8)), reads=[bbank[bi]], writes=[bxT[t]])
                rl = max(t0n - 1, 0); rr = min(t0n + G, T - 1)
                S.dma("sp", lambda e: e.dma_start(out=xh[0:1, :], in_=xin[rl:rl + 1, :]), writes=[bxh])
                S.dma("sp", lambda e: e.dma_start(out=xh[1:2, :], in_=xin[rr:rr + 1, :]), writes=[bxh])
                S.dma("sp", lambda e: e.dma_start(out=sbg[:], in_=SBd[gn * TPG:(gn + 1) * TPG].rearrange("t p f -> p t f")), writes=[bsbg])
                S.op("dve", lambda e: e.tensor_scalar(out=xh16[:], in0=xh[:], scalar1=hmask[:, gn:gn + 1], scalar2=None, op0=ALU.mult), reads=[bxh, bC], writes=[bxh16])
                pb = bank_bf16(6)
                for k in range(8):
                    S.op("pe", lambda e, k=k: e.transpose(out=pb[:, k * 2:(k + 1) * 2], in_=xh16[:, k * 128:(k + 1) * 128], identity=ident[0:2, 0:2]),
                         reads=[bxh16, bC], writes=[bbank[6]], inc=(k == 7))
                S.op("dve", lambda e: e.tensor_copy(out=xTh[:].rearrange("p k t -> p (k t)"), in_=pb[:, 0:16]), reads=[bbank[6]], writes=[bxTh])
                for t in range(TPG):
                    for k in range(8):
                        S.op("pe", lambda e, k=k, t=t: e.matmul(bank[7][:, t * 16:(t + 1) * 16], lhsT=xT[:, k, t * 128:(t + 1) * 128], rhs=wg[:, k, :], start=(k == 0), stop=(k == 7)),
                             reads=[bxT[t], bwg], writes=[bbank[7]], inc=(k == 7))
                gate_scalars(TPG, bank[7][:, 0:TPG * 16].rearrange("p (t c) -> p t c", t=TPG), bbank[7], True)
                for t in range(TPG):
                    o = 64 + t * 24
                    S.op("pe", lambda e, t=t, o=o: e.matmul(bank[7][:, o:o + 4], lhsT=UT[:], rhs=nlf[:, t, 0:4], start=True, stop=True), reads=[bnlf, bC], writes=[bbank[7]], inc=False)
                    S.op("pe", lambda e, t=t, o=o: e.matmul(bank[7][:, o + 4:o + 8], lhsT=LT[:], rhs=nlf[:, t, 4:8], start=True, stop=True), reads=[bnlf, bC], writes=[bbank[7]], inc=False)
                    S.op("pe", lambda e, t=t, o=o: e.matmul(bank[7][:, o + 8:o + 12], lhsT=NSG[:], rhs=nlf[:, t, 0:4], start=True, stop=True), reads=[bnlf, bC], writes=[bbank[7]], inc=False)
                    S.op("pe", lambda e, t=t, o=o: e.matmul(bank[7][:, o + 12:o + 16], lhsT=NSL[:], rhs=nlf[:, t, 4:8], start=True, stop=True), reads=[bnlf, bC], writes=[bbank[7]], inc=False)
                    S.op("pe", lambda e, t=t, o=o: e.matmul(bank[7][:, o + 16:o + 24], lhsT=ONES[:], rhs=nlf[:, t, :], start=True, stop=True), reads=[bnlf, bC], writes=[bbank[7]], inc=(t == TPG - 1))
                b3 = bank[7][:, 64:64 + TPG * 24].rearrange("p (t c) -> p t c", t=TPG)
                igv = zg[:].rearrange("p t (d g h) -> p t d g h", d=2, g=2)[:, :, :, 0, :]
                S.op("act", lambda e: e.activation(out=eq[:], in_=b3[:, :, 0:8], func=AF.Exp, scale=-1.0, bias=LNQ), reads=[bbank[7]], writes=[bsc])
                S.op("dve", lambda e: e.tensor_tensor(out=tmp8[:].rearrange("p t (d h) -> p t d h", d=2), in0=b3[:, :, 0:8].rearrange("p t (d h) -> p t d h", d=2), in1=igv, op=ALU.add), reads=[bbank[7], bzg], writes=[btmp8])
                S.op("act", lambda e: e.activation(out=ek[:], in_=tmp8[:], func=AF.Exp), reads=[btmp8], writes=[bsc])
                S.op("dve", lambda e: e.tensor_tensor(out=tmp8[:].rearrange("p t (d h) -> p t d h", d=2), in0=b3[:, :, 8:16].rearrange("p t (d h) -> p t d h", d=2), in1=igv, op=ALU.add), reads=[bbank[7], bzg, bsc], writes=[btmp8])
                S.op("act", lambda e: e.activation(out=ekk[:], in_=tmp8[:], func=AF.Exp), reads=[btmp8], writes=[bsc])
                S.op("act", lambda e: e.activation(out=eG[:], in_=b3[:, :, 16:24], func=AF.Exp, scale=-1.0), reads=[bbank[7]], writes=[bsc])

            prologue(0)
            pend_ln2 = []
            for g in range(NG):
                t0 = g * G
                def load_xg(t0=t0):
                    for t in range(TPG):
                        S.dma("sp", lambda e, t=t: e.dma_start(out=xg[:, t, :], in_=xin[t0 + t * 128:t0 + (t + 1) * 128, :]), writes=[bxg[t]])

                items = [(h, t) for h in range(NH) for t in range(TPG)]
                NI = len(items)
                wts = {}

                def st1(i):
                    h, t = items[i]
                    if t == 0:
                        wts[h] = load_piece(l, WH[l][h])
                    wt, bwt = wts[h]
                    q_ = i % NQ; qb_ = i % NQB; pbk = i % 3
                    for k in range(8):
                        S.op("pe", lambda e, k=k: e.matmul(bank[pbk][:, :], lhsT=xT[:, k, t * 128:(t + 1) * 128], rhs=wt[:, k, :], start=(k == 0), stop=(k == 7)),
                             reads=[bxT[t], bwt], writes=[bbank[pbk]], inc=(k == 7))
                    pj = bank[pbk]
                    S.op("act", lambda e: e.activation(out=qk[q_][:, 0, :], in_=pj[:, 0:128], func=AF.Copy, scale=eq[:, t, h:h + 1]), reads=[bbank[pbk], bsc], writes=[bqk[q_]])
                    S.op("act", lambda e: e.activation(out=qk[q_][:, 1, :], in_=pj[:, 0:128], func=AF.Copy, scale=eq[:, t, 4 + h:5 + h]), reads=[bbank[pbk], bsc], writes=[bqk[q_]])
                    S.op("act", lambda e: e.activation(out=qk[q_][:, 2, :], in_=pj[:, 128:256], func=AF.Copy, scale=ek[:, t, h:h + 1]), reads=[bbank[pbk], bsc], writes=[bqk[q_]])
                    S.op("dve", lambda e: e.tensor_scalar(out=qk[q_][:, 3, :], in0=pj[:, 128:256], scalar1=ek[:, t, 4 + h:5 + h], scalar2=None, op0=ALU.mult), reads=[bbank[pbk], bsc], writes=[bqk[q_]])
                    S.op("dve", lambda e: e.tensor_scalar(out=k2[q_][:], in0=pj[:, 128:256], scalar1=ekk[:, t, h:h + 1], scalar2=None, op0=ALU.mult), reads=[bbank[pbk], bsc], writes=[bk2[q_]])
                    S.op("act", lambda e: e.copy(out=vx[q_][:, 0:128], in_=pj[:, 256:384]), reads=[bbank[pbk]], writes=[bvx[q_]])
                    S.op("act", lambda e: e.activation(out=sg[qb_][:], in_=pj[:, 384:512], func=AF.Exp, scale=-1.0), reads=[bbank[pbk]], writes=[bsg[qb_]])
                    S.op("act", lambda e: e.activation(out=sg[qb_][:], in_=sg[qb_][:], func=AF.Ln, bias=1.0, scale=1.0), reads=[bsg[qb_]], writes=[bsg[qb_]])
                    S.op("act", lambda e: e.activation(out=sg[qb_][:], in_=sg[qb_][:], func=AF.Exp, scale=-1.0), reads=[bsg[qb_]], writes=[bsg[qb_]])

                def st2(i):
                    q_ = i % NQ; r_ = i % 2
                    pT = bank_bf16(3)[:, r_ * 512:(r_ + 1) * 512]
                    for a_ in range(4):
                        S.op("pe", lambda e, a_=a_: e.transpose(out=pT[:, a_ * 128:(a_ + 1) * 128], in_=qk[q_][:, a_, :], identity=ident[:]),
                             reads=[bqk[q_], bC], writes=[Rtq[r_]], inc=(a_ == 3))
                    S.op("act", lambda e: e.copy(out=qkT[q_][:].rearrange("p a t -> p (a t)"), in_=pT), reads=[Rtq[r_]], writes=[bqkT[q_]])

                def st3(i):
                    q_ = i % NQ; r_ = i % 2
                    pS = bank[4][:, r_ * 256:(r_ + 1) * 256]
                    for d_ in range(2):
                        S.op("pe", lambda e, d_=d_: e.matmul(pS[:, d_ * 128:(d_ + 1) * 128], lhsT=qkT[q_][:, 2 + d_, :], rhs=qkT[q_][:, d_, :], start=True, stop=True),
                             reads=[bqkT[q_]], writes=[Rst[r_]], inc=(d_ == 1))
                    S.op("dve", lambda e: e.tensor_tensor(out=st16[q_][:], in0=pS.rearrange("p (d t) -> p d t", d=2), in1=mask2[:], op=ALU.mult), reads=[Rst[r_], bC], writes=[bst16[q_]])

                def st4(i):
                    h, t = items[i]
                    c = g * TPG + t
                    q_ = i % NQ; qb_ = i % NQB; ob = 5 + (i % 2)
                    if c == MIDC and h == 0:
                        S.op("dve", lambda e: e.tensor_scalar(out=CTf[:], in0=CTf[:], scalar1=brk[:, 0:1], scalar2=None, op0=ALU.mult), reads=bCTf + [bC], writes=bCTf)
                        S.op("act", lambda e: e.copy(out=CTf16[:], in_=CTf[:]), reads=bCTf, writes=bCTf16)
                    po = bank[ob]
                    S.op("pe", lambda e: e.matmul(po[:, 0:129], lhsT=st16[q_][:, 0, :], rhs=vx[q_][:], start=True, stop=False), reads=[bst16[q_], bvx[q_]], writes=[bbank[ob]], inc=False)
                    S.op("pe", lambda e: e.matmul(po[:, 0:129], lhsT=qkT[q_][:, 0, :], rhs=CTf16[:, h, :], start=False, stop=True), reads=[bqkT[q_], bCTf16[h]], writes=[bbank[ob]], inc=False)
                    S.op("pe", lambda e: e.matmul(po[:, 129:258], lhsT=st16[q_][:, 1, :], rhs=vx[q_][:], start=True, stop=False), reads=[bst16[q_], bvx[q_]], writes=[bbank[ob]], inc=False)
                    S.op("pe", lambda e: e.matmul(po[:, 129:258], lhsT=qkT[q_][:, 1, :], rhs=sbg[:, t, h * 129:(h + 1) * 129], start=False, stop=True), reads=[bqkT[q_], bsbg], writes=[bbank[ob]], inc=False)
                    S.op("pe", lambda e: e.matmul(po[:, 258:387], lhsT=k2[q_][:], rhs=vx[q_][:], start=True, stop=True), reads=[bk2[q_], bvx[q_]], writes=[bbank[ob]])
                    S.op("dve", lambda e: e.scalar_tensor_tensor(out=CTf[:, h, :], in0=CTf[:, h, :], scalar=eG[:, t, h:h + 1], in1=po[:, 258:387], op0=ALU.mult, op1=ALU.add),
                         reads=[bCTf[h], bsc, bbank[ob]], writes=[bCTf[h]])
                    smq = sm[qb_]; bsmq = bsm[qb_]
                    dens = po[:, 0:258].rearrange("p (d e) -> p d e", d=2)[:, :, 128]
                    S.op("dve", lambda e: e.tensor_scalar(out=smq[:, 0:2], in0=dens, scalar1=-1.0, scalar2=1.0, op0=ALU.mult, op1=ALU.max), reads=[bbank[ob]], writes=[bsmq])
                    S.op("dve", lambda e: e.scalar_tensor_tensor(out=smq[:, 0:2], in0=dens, scalar=1.0, in1=smq[:, 0:2], op0=ALU.max, op1=ALU.max), reads=[bbank[ob], bsmq], writes=[bsmq])
                    S.op("dve", lambda e: e.reciprocal(out=smq[:, 2:4], in_=smq[:, 0:2]), reads=[bsmq], writes=[bsmq])
                    S.op("dve", lambda e: e.tensor_scalar(out=hs[qb_][:], in0=po[:, 0:128], scalar1=smq[:, 2:3], scalar2=None, op0=ALU.mult), reads=[bbank[ob], bsmq], writes=[bhs[qb_]])
                    S.op("dve", lambda e: e.scalar_tensor_tensor(out=hs[qb_][:], in0=po[:, 129:257], scalar=smq[:, 3:4], in1=hs[qb_][:], op0=ALU.mult, op1=ALU.add),
                         reads=[bbank[ob], bsmq, bhs[qb_]], writes=[bhs[qb_]])
                    S.op("dve", lambda e: e.bn_stats(out=smq[:, 4:10], in_=hs[qb_][:]), reads=[bhs[qb_], bsmq], writes=[bsmq])
                    S.op("dve", lambda e: e.bn_aggr(out=smq[:, 10:12], in_=smq[:, 4:10]), reads=[bsmq], writes=[bsmq])
                    S.op("dve", lambda e: e.tensor_scalar(out=smq[:, 12:13], in0=smq[:, 11:12], scalar1=EPS, scalar2=None, op0=ALU.add), reads=[bsmq], writes=[bsmq])
                    S.op("act", lambda e: e.copy(out=CTf16[:, h, :], in_=CTf[:, h, :]), reads=[bCTf[h]], writes=[bCTf16[h]])

                def st4b_act(i):
                    qb_ = i % NQB
                    smq = sm[qb_]; bsmq = bsm[qb_]
                    S.op("act", lambda e: e.activation(out=smq[:, 12:13], in_=smq[:, 12:13], func=AF.Ln), reads=[bsmq], writes=[bsmq])
                    S.op("act", lambda e: e.activation(out=smq[:, 13:14], in_=smq[:, 12:13], func=AF.Exp, scale=-0.5), reads=[bsmq], writes=[bsmq])

                def st4b(i):
                    qb_ = i % NQB
                    smq = sm[qb_]; bsmq = bsm[qb_]
                    S.op("dve", lambda e: e.tensor_scalar(out=hs[qb_][:], in0=hs[qb_][:], scalar1=smq[:, 10:11], scalar2=smq[:, 13:14], op0=ALU.subtract, op1=ALU.mult), reads=[bhs[qb_], bsmq], writes=[bhs[qb_]])
                    S.op("dve", lambda e: e.tensor_tensor(out=gm[qb_][:], in0=hs[qb_][:], in1=sg[qb_][:], op=ALU.mult), reads=[bhs[qb_], bsg[qb_]], writes=[bgm[qb_]])

                def st5(i):
                    h, t = items[i]
                    qb_ = i % NQB; r_ = i % 2
                    pT2 = bank_bf16(7)[:, r_ * 128:(r_ + 1) * 128]
                    S.op("pe", lambda e: e.transpose(out=pT2, in_=gm[qb_][:], identity=ident[:]), reads=[bgm[qb_], bC], writes=[Rth[r_]])
                    S.op("act", lambda e: e.activation(out=hmT[:, h, t * 128:(t + 1) * 128], in_=pT2, func=AF.Copy, scale=mhw[:, h:h + 1]), reads=[Rth[r_], bPar], writes=[bhmT[h][t]])

                conv_w = {}

                def conv_mm(j):
                    wt, bwt = load_piece(l, WC[l][j])
                    conv_w[j] = (wt, bwt)
                    bs_ = (0, 1, 2) if j % 2 == 0 else (3, 4, 5)
                    for i in range(3):
                        for k in range(8):
                            S.op("pe", lambda e, k=k, i=i: e.matmul(bank[bs_[i]][:, :], lhsT=wt[:, k, i * 128:(i + 1) * 128], rhs=xT[:, k, :], start=(k == 0), stop=(k == 7)),
                                 reads=bxT + [bwt], writes=[bbank[bs_[i]]], inc=(k == 7))
                    ho = 256 + (j % 2) * 4
                    for i in (1, 2):
                        for k in range(8):
                            S.op("pe", lambda e, k=k, i=i: e.matmul(bank[7][:, ho + (i - 1) * 2:ho + (i - 1) * 2 + 2], lhsT=wt[:, k, i * 128:(i + 1) * 128], rhs=xTh[:, k, :], start=(k == 0), stop=(k == 7)),
                                 reads=[bxTh, bwt], writes=[bbank[7]], inc=(k == 7 and i == 2))

                def conv_ev(j):
                    bs_ = (0, 1, 2) if j % 2 == 0 else (3, 4, 5)
                    ho = 256 + (j % 2) * 4
                    S.op("act", lambda e: e.copy(out=ccs[:, 1:G + 1], in_=bank[bs_[1]][:, :]), reads=[bbank[bs_[1]]], writes=[bccs])
                    S.op("act", lambda e: e.copy(out=hh[:], in_=bank[7][:, ho:ho + 4]), reads=[bbank[7]], writes=[bhh])
                    S.op("dve", lambda e: e.tensor_tensor(out=uu[:, 1:G + 1], in0=ccs[:, 1:G + 1], in1=bank[bs_[2]][:, :], op=ALU.mult), reads=[bccs, bbank[bs_[2]]], writes=[buu])
                    S.op("dve", lambda e: e.tensor_tensor(out=uu[:, 0:1], in0=hh[:, 0:1], in1=hh[:, 2:3], op=ALU.mult), reads=[bhh, buu], writes=[buu])
                    S.op("dve", lambda e: e.tensor_tensor(out=uu[:, G + 1:G + 2], in0=hh[:, 1:2], in1=hh[:, 3:4], op=ALU.mult), reads=[bhh, buu], writes=[buu])
                    S.op("dve", lambda e: e.tensor_scalar(out=yy[:], in0=uu[:, 0:G], scalar1=cw[:, j, 0:1], scalar2=None, op0=ALU.mult), reads=[buu, bPar], writes=[byy])
                    S.op("dve", lambda e: e.scalar_tensor_tensor(out=yy[:], in0=uu[:, 1:G + 1], scalar=cw[:, j, 1:2], in1=yy[:], op0=ALU.mult, op1=ALU.add), reads=[buu, bPar, byy], writes=[byy])
                    S.op("dve", lambda e: e.scalar_tensor_tensor(out=yy[:], in0=uu[:, 2:G + 2], scalar=cw[:, j, 2:3], in1=yy[:], op0=ALU.mult, op1=ALU.add), reads=[buu, bPar, byy], writes=[byy])
                    S.op("dve", lambda e: e.tensor_tensor(out=hmT[:, 4 + j, :], in0=yy[:], in1=bank[bs_[0]][:, :], op=ALU.mult), reads=[byy, bbank[bs_[0]]], writes=bhmT[4 + j])

                conv_sched = {1: [("mm", 0)], 3: [("mm", 1), ("ev", 0)], 4: [("mm", 2)], 5: [("ev", 1)], 6: [("mm", 3), ("ev", 2)], 8: [("ev", 3)]}

                for s_ in range(NI + 9):
                    if 0 <= s_ - 4 < NI:
                        st4(s_ - 4)
                    if 0 <= s_ - 5 < NI:
                        st4b_act(s_ - 5)
                    if 0 <= s_ - 8 < NI:
                        st5(s_ - 8)
                    if 0 <= s_ - 2 < NI:
                        st2(s_ - 2)
                    if 0 <= s_ - 3 < NI:
                        st3(s_ - 3)
                    if 0 <= s_ - 5 < NI:
                        st4b(s_ - 5)
                    if s_ < NI:
                        st1(s_)
                    if s_ in (2, 4, 6, 8) and pend_ln2:
                        pend_ln2.pop(0)()
                    if s_ == 9:
                        load_xg()
                    for kind, j in conv_sched.get(s_ - NI, ()):
                        (conv_mm if kind == "mm" else conv_ev)(j)

                wo = [load_piece(l, WO[l][j]) for j in range(2)]

                def outproj(t):
                    for j in range(2):
                        bi = (t % 2) * 2 + j
                        for k in range(8):
                            S.op("pe", lambda e, k=k, j=j, bi=bi: e.matmul(bank[bi][:, :], lhsT=hmT[:, k, t * 128:(t + 1) * 128], rhs=wo[j][0][:, k, :], start=(k == 0), stop=(k == 7)),
                                 reads=[bhmT[k][t], wo[j][1]], writes=[bbank[bi]], inc=(k == 7))
                        S.op("dve", lambda e, j=j, bi=bi: e.scalar_tensor_tensor(out=xg[:, t, j * 512:(j + 1) * 512], in0=xg[:, t, j * 512:(j + 1) * 512], scalar=ALPHA, in1=bank[bi][:, :], op0=ALU.mult, op1=ALU.add),
                             reads=[bxg[t], bbank[bi]], writes=[bxg[t]])
                    ln_rows(xg[:, t, :], bxg[t], g1, b1, smL[t], bsmL[t])
                    S.op("act", lambda e: e.copy(out=xb16a[t % 2][:], in_=xg[:, t, :]), reads=[bxg[t]], writes=[bxb16a[t % 2]])

                def x1_transpose(t):
                    pbi = 4 + (t % 2)
                    pb = bank_bf16(pbi)
                    for k in range(8):
                        S.op("pe", lambda e, k=k, pb=pb: e.transpose(out=pb[:, k * 128:(k + 1) * 128], in_=xb16a[t % 2][:, k * 128:(k + 1) * 128], identity=ident[:]),
                             reads=[bxb16a[t % 2], bC], writes=[bbank[pbi]], inc=(k == 7))
                    S.op("act", lambda e, pb=pb: e.copy(out=x1T[:, :, t * 128:(t + 1) * 128], in_=pb.rearrange("p (k t) -> p k t", k=8)), reads=[bbank[pbi]], writes=[bx1T[t]])

                for t in range(TPG + 1):
                    if t < TPG:
                        outproj(t)
                    if t >= 1:
                        x1_transpose(t - 1)

                for p in range(8):
                    if p == 4 and g + 1 < NG:
                        prologue(g + 1)
                    wt, bwt = load_piece(l, W1[l][p])
                    for fc in range(4):
                        f = 4 * p + fc
                        bi = f % 4
                        for k in range(8):
                            S.op("pe", lambda e, k=k, fc=fc, bi=bi: e.matmul(bank[bi][:, :], lhsT=wt[:, k, fc * 128:(fc + 1) * 128], rhs=x1T[:, k, :], start=(k == 0), stop=(k == 7)),
                                 reads=bx1T + [bwt], writes=[bbank[bi]], inc=(k == 7))
                        r_ = f % 2
                        S.op("act", lambda e, bi=bi, r_=r_: e.activation(out=relu_t[r_][:], in_=bank[bi][:, :], func=AF.Relu), reads=[bbank[bi]], writes=[brelu[r_]])
                        eng2 = "pool" if (f % 2) else "dve"
                        S.op(eng2, lambda e, f=f, r_=r_: e.tensor_tensor(out=h1T[:, f, :], in0=relu_t[r_][:], in1=relu_t[r_][:], op=ALU.mult), reads=[brelu[r_]], writes=[bh1T[f]])

                for p in range(8):
                    wt, bwt = load_piece(l, W2[l][p], shape3=True)
                    for t in range(TPG):
                        for j in range(2):
                            bi = t * 2 + j
                            for fc in range(4):
                                f = 4 * p + fc
                                S.op("pe", lambda e, t=t, j=j, bi=bi, fc=fc, f=f, p=p: e.matmul(bank[bi][:, :], lhsT=h1T[:, f, t * 128:(t + 1) * 128], rhs=wt[:, fc, j * 512:(j + 1) * 512],
                                                                                             start=(p == 0 and fc == 0), stop=(p == 7 and fc == 3)),
                                     reads=[bh1T[f], bwt], writes=[bbank[bi]], inc=(fc == 3))
                for t in range(TPG):
                    for j in range(2):
                        bi = t * 2 + j
                        S.op("dve", lambda e, t=t, j=j, bi=bi: e.scalar_tensor_tensor(out=xg[:, t, j * 512:(j + 1) * 512], in0=xg[:, t, j * 512:(j + 1) * 512], scalar=ALPHA, in1=bank[bi][:, :], op0=ALU.mult, op1=ALU.add),
                             reads=[bxg[t], bbank[bi]], writes=[bxg[t]])

                def ln2_rest(t, t0=t0):
                    ln_rows(xg[:, t, :], bxg[t], g2, b2, smL[t], bsmL[t])
                    S.dma("pool", lambda e: e.dma_start(out=xout[t0 + t * 128:t0 + (t + 1) * 128, :], in_=xg[:, t, :]), reads=[bxg[t]])

                for t in range(TPG):
                    pend_ln2.append(lambda t=t, f=ln2_rest: f(t))
                if g == NG - 1:
                    while pend_ln2:
                        pend_ln2.pop(0)()
        S.finish()
    return nc, S


def _run(inputs, T, depth, seqs):
    nc, S = build_program(T, depth)
    NG = T // G
    in_maps = []
    for ci in range(8):
        x, brkv = seqs[ci]
        hm = np.ones((2, NG), np.float32)
        hm[0, 0] = 0.0
        hm[1, NG - 1] = 0.0
        hm[0, NG // 2] = brkv
        hm[1, NG // 2 - 1] = brkv
        m = {"x": np.ascontiguousarray(x, dtype=np.float32),
             "brk": np.full((128, 1), brkv, np.float32),
             "hmask": hm,
             "w_in": np.ascontiguousarray(inputs["w_in"][:depth]),
             "b_gate": np.ascontiguousarray(inputs["b_gate"][:depth]).reshape(depth, 16),
             "mh_norm_w": np.ascontiguousarray(inputs["mh_norm_w"][:depth]),
             "conv_w": np.ascontiguousarray(inputs["conv_w"][:depth]),
             "w_out": np.ascontiguousarray(inputs["w_out"][:depth]),
             "ln1_g": np.ascontiguousarray(inputs["ln1_g"][:depth]),
             "ln1_b": np.ascontiguousarray(inputs["ln1_b"][:depth]),
             "w_ff1": np.ascontiguousarray(inputs["w_ff1"][:depth]),
             "w_ff2": np.ascontiguousarray(inputs["w_ff2"][:depth]),
             "ln2_g": np.ascontiguousarray(inputs["ln2_g"][:depth]),
             "ln2_b": np.ascontiguousarray(inputs["ln2_b"][:depth])}
        in_maps.append(m)
    res = run_bass_kernel_spmd(nc, in_maps, core_ids=list(range(8)))
    return [r["y"] for r in res.results]


def kernel(x_prompt, x_sample, w_in, b_gate, mh_norm_w, conv_w, w_out,
           ln1_g, ln1_b, w_ff1, w_ff2, ln2_g, ln2_b):
    inputs = dict(w_in=np.asarray(w_in, np.float32), b_gate=np.asarray(b_gate, np.float32), mh_norm_w=np.asarray(mh_norm_w, np.float32),
                  conv_w=np.asarray(conv_w, np.float32), w_out=np.asarray(w_out, np.float32), ln1_g=np.asarray(ln1_g, np.float32),
                  ln1_b=np.asarray(ln1_b, np.float32), w_ff1=np.asarray(w_ff1, np.float32), w_ff2=np.asarray(w_ff2, np.float32),
                  ln2_g=np.asarray(ln2_g, np.float32), ln2_b=np.asarray(ln2_b, np.float32))
    xp = np.asarray(x_prompt, np.float32)
    xsm = np.asarray(x_sample, np.float32)
    T = xp.shape[1]
    assert xsm.shape[0] * xsm.shape[1] == T
    zero = np.zeros((T, D), np.float32)
    seqs = [(xp[0], 1.0), (xsm.reshape(T, D), 0.0)] + [(zero, 0.0)] * 6
    ys = _run(inputs, T, DEPTH, seqs)
    y_prompt = ys[0].reshape(xp.shape).astype(np.float32)
    y_sample = ys[1].reshape(xsm.shape).astype(np.float32)
    return (y_prompt, y_sample)
```

```python
import numpy as np
from contextlib import ExitStack
import concourse.bass as bass
import concourse.mybir as mybir
from concourse.bass_utils import run_bass_kernel_spmd

F32 = mybir.dt.float32
BF16 = mybir.dt.bfloat16
AF = mybir.ActivationFunctionType
ALU = mybir.AluOpType

D = 1024
DEPTH = 4
NH = 4
DH = 128
PROJ = 3600
DFF = 4096
ALPHA = (2.0 * DEPTH) ** 0.25
EPS = 1e-5
G = 512
TPG = 4


class Buf:
    __slots__ = ("name", "w", "r")

    def __init__(self, name):
        self.name = name
        self.w = None
        self.r = []


class Sched:
    def __init__(self, nc, es, n_dma_sems=24):
        self.nc = nc
        self.eng = {"pe": nc.tensor, "act": nc.scalar, "dve": nc.vector, "pool": nc.gpsimd, "sp": nc.sync}
        self.sem = {e: es.enter_context(nc.semaphore("c_" + e)) for e in ("pe", "act", "dve", "pool")}
        self.cnt = {e: 0 for e in self.sem}
        self.seen = {e: {} for e in self.eng}
        self.dsem = {}
        for q, n in (("sp", n_dma_sems), ("pool", 48)):
            self.dsem[q] = [[es.enter_context(nc.semaphore("d_%s%d" % (q, i))), 0] for i in range(n)]
        self.dnext = {"sp": 0, "pool": 0}
        self.n_inst = 0
        self.n_wait = 0

    def _wait(self, e, tok):
        if tok is None:
            return
        sem, val, key = tok
        if key == e and e == "pe":
            return
        if self.seen[e].get(key, 0) >= val:
            return
        self.eng[e].wait_ge(sem, val)
        self.seen[e][key] = val
        self.n_wait += 1

    def _deps(self, e, reads, writes):
        for b in reads:
            self._wait(e, b.w)
        for b in writes:
            self._wait(e, b.w)
            for t in b.r:
                self._wait(e, t)

    def _commit(self, tok, reads, writes):
        for b in reads:
            b.r.append(tok)
            if len(b.r) > 6:
                d = {}
                for t in b.r:
                    if t[2] not in d or d[t[2]][1] < t[1]:
                        d[t[2]] = t
                b.r = list(d.values())
        for b in writes:
            b.w = tok
            b.r = []

    def _flat(self, bs):
        out = []
        for b in bs:
            if isinstance(b, (list, tuple)):
                out.extend(self._flat(b))
            else:
                out.append(b)
        return out

    def op(self, e, fn, reads=(), writes=(), inc=True):
        reads = self._flat(reads); writes = self._flat(writes)
        self._deps(e, reads, writes)
        inst = fn(self.eng[e])
        self.n_inst += 1
        if inc:
            self.cnt[e] += 1
            inst.then_inc(self.sem[e], 1)
            tok = (self.sem[e], self.cnt[e], e)
        else:
            tok = (self.sem[e], self.cnt[e] + 1, e)
        self._commit(tok, reads, writes)
        return tok

    def dma(self, q, fn, reads=(), writes=()):
        slot = self.dnext[q]
        self.dnext[q] = (slot + 1) % len(self.dsem[q])
        ent = self.dsem[q][slot]
        key = "%s%d" % (q, slot)
        reads = self._flat(reads); writes = self._flat(writes)
        if ent[1] > 0:
            self._wait(q, (ent[0], ent[1], key))
        self._deps(q, reads, writes)
        inst = fn(self.eng[q])
        self.n_inst += 1
        ent[1] += 16
        inst.then_inc(ent[0], 16)
        tok = (ent[0], ent[1], key)
        self._commit(tok, reads, writes)
        return tok

    def barrier(self, waiter, on):
        for i, ent in enumerate(self.dsem[on]):
            if ent[1] > 0:
                self._wait(waiter, (ent[0], ent[1], "%s%d" % (on, i)))

    def finish(self):
        for q in ("sp", "pool"):
            for i, ent in enumerate(self.dsem[q]):
                if ent[1] > 0:
                    self._wait(q, (ent[0], ent[1], "%s%d" % (q, i)))


def build_program(T, depth):
    NCH = T // 128
    NG = T // G
    MIDC = NCH // 2
    MIDG = NG // 2
    nc = bass.Bass("TRN2", target_bir_lowering=False)
    x_in = nc.dram_tensor("x", [T, D], F32, kind="ExternalInput").ap()
    brk_in = nc.dram_tensor("brk", [128, 1], F32, kind="ExternalInput").ap()
    hmask_in = nc.dram_tensor("hmask", [2, NG], F32, kind="ExternalInput").ap()
    w_in = nc.dram_tensor("w_in", [depth, D, PROJ], F32, kind="ExternalInput").ap()
    b_gate = nc.dram_tensor("b_gate", [depth, 16], F32, kind="ExternalInput").ap()
    mhw_in = nc.dram_tensor("mh_norm_w", [depth, 512], F32, kind="ExternalInput").ap()
    convw_in = nc.dram_tensor("conv_w", [depth, 3, 512], F32, kind="ExternalInput").ap()
    w_out = nc.dram_tensor("w_out", [depth, D, D], F32, kind="ExternalInput").ap()
    ln1g = nc.dram_tensor("ln1_g", [depth, D], F32, kind="ExternalInput").ap()
    ln1b = nc.dram_tensor("ln1_b", [depth, D], F32, kind="ExternalInput").ap()
    w_ff1 = nc.dram_tensor("w_ff1", [depth, D, DFF], F32, kind="ExternalInput").ap()
    w_ff2 = nc.dram_tensor("w_ff2", [depth, DFF, D], F32, kind="ExternalInput").ap()
    ln2g = nc.dram_tensor("ln2_g", [depth, D], F32, kind="ExternalInput").ap()
    ln2b = nc.dram_tensor("ln2_b", [depth, D], F32, kind="ExternalInput").ap()
    y_out = nc.dram_tensor("y", [T, D], F32, kind="ExternalOutput").ap()

    def dscr(name, shape, dt):
        return nc.dram_tensor(name, shape, dt, kind="Internal").ap()

    xs = [dscr("xs0", [T, D], F32), dscr("xs1", [T, D], F32)]
    SBd = dscr("sbst", [NCH, 128, NH * 129], BF16)
    WA = [dscr("WA%d" % l, [128, 8, 1040], BF16) for l in range(depth)]
    WH = [[dscr("WH%d_%d" % (l, h), [128, 8, 512], BF16) for h in range(NH)] for l in range(depth)]
    WG = [dscr("WG%d" % l, [128, 8, 16], BF16) for l in range(depth)]
    WC = [[dscr("WC%d_%d" % (l, j), [128, 8, 384], BF16) for j in range(4)] for l in range(depth)]
    WO = [[dscr("WO%d_%d" % (l, j), [128, 8, 512], BF16) for j in range(2)] for l in range(depth)]
    W1 = [[dscr("W1%d_%d" % (l, j), [128, 8, 512], BF16) for j in range(8)] for l in range(depth)]
    W2 = [[dscr("W2%d_%d" % (l, j), [128, 4, 1024], BF16) for j in range(8)] for l in range(depth)]

    bXl = [Buf("xl")]
    bXn = [Buf("xn")]
    bSB = [Buf("sbd")]
    with ExitStack() as es:
        S = Sched(nc, es)

        def sb(name, shape, dt):
            return es.enter_context(nc.sbuf_tensor("s_" + name, shape, dt))

        def ps(name, shape, dt):
            return es.enter_context(nc.psum_tensor("p_" + name, shape, dt))

        ident = sb("ident", [128, 128], BF16)
        UT = sb("UT", [128, 128], F32)
        LT = sb("LT", [128, 128], F32)
        NSG = sb("NSG", [128, 128], F32)
        NSL = sb("NSL", [128, 128], F32)
        ONES = sb("ONES", [128, 128], F32)
        mask2 = sb("mask2", [128, 2, 128], F32)
        brk = sb("brk", [128, 1], F32)
        hmask = sb("hmask", [2, NG], F32)
        bC = Buf("const")

        def cst(fn):
            S.op("pool", fn, writes=[bC])

        cst(lambda e: e.memset(ident[:], 0.0))
        cst(lambda e: e.affine_select(out=ident[:], in_=ident[:], pattern=[[-1, 128]], compare_op=ALU.not_equal, fill=1.0, base=0, channel_multiplier=1))
        cst(lambda e: e.memset(ONES[:], 1.0))
        cst(lambda e: e.memset(UT[:], 1.0))
        cst(lambda e: e.affine_select(out=UT[:], in_=UT[:], pattern=[[1, 128]], compare_op=ALU.is_ge, fill=0.0, base=0, channel_multiplier=-1))
        cst(lambda e: e.memset(LT[:], 1.0))
        cst(lambda e: e.affine_select(out=LT[:], in_=LT[:], pattern=[[-1, 128]], compare_op=ALU.is_ge, fill=0.0, base=0, channel_multiplier=1))
        cst(lambda e: e.memset(NSG[:], -1.0))
        cst(lambda e: e.affine_select(out=NSG[:], in_=NSG[:], pattern=[[-1, 128]], compare_op=ALU.is_gt, fill=0.0, base=0, channel_multiplier=1))
        cst(lambda e: e.memset(NSL[:], -1.0))
        cst(lambda e: e.affine_select(out=NSL[:], in_=NSL[:], pattern=[[1, 128]], compare_op=ALU.is_gt, fill=0.0, base=0, channel_multiplier=-1))
        cst(lambda e: e.tensor_copy(out=mask2[:, 0, :], in_=UT[:]))
        cst(lambda e: e.tensor_copy(out=mask2[:, 1, :], in_=LT[:]))
        S.dma("sp", lambda e: e.dma_start(out=brk[:], in_=brk_in), writes=[bC])
        S.dma("sp", lambda e: e.dma_start(out=hmask[:], in_=hmask_in), writes=[bC])

        bW = [Buf("wconv%d" % l) for l in range(depth)]

        def conv_dma(l, dst, src):
            S.dma("pool", lambda e: e.dma_start(out=dst, in_=src))

        def wsrc(w2d, c0, n):
            return w2d[:, c0:c0 + n].rearrange("(k p) n -> p k n", p=128)

        def convert_A(l):
            wi = w_in[l]
            conv_dma(l, WA[l][:, :, 0:512], wsrc(wi, 512, 512))
            conv_dma(l, WA[l][:, :, 512:1024], wsrc(wi, 1024, 512))
            conv_dma(l, WA[l][:, :, 1024:1040], wsrc(wi, 2048, 16))
            conv_dma(l, WG[l][:, :, :], wsrc(wi, 2048, 16))

        def convert_layer(l):
            wi = w_in[l]
            for h in range(NH):
                for i, base in enumerate((0, 512, 1024, 1536)):
                    conv_dma(l, WH[l][h][:, :, i * 128:(i + 1) * 128], wsrc(wi, base + h * 128, 128))
            for j in range(4):
                for i, base in enumerate((2064, 2576, 3088)):
                    conv_dma(l, WC[l][j][:, :, i * 128:(i + 1) * 128], wsrc(wi, base + j * 128, 128))
            for j in range(2):
                conv_dma(l, WO[l][j][:, :, :], wsrc(w_out[l], j * 512, 512))
            for j in range(8):
                conv_dma(l, W1[l][j][:, :, :], wsrc(w_ff1[l], j * 512, 512))
            for j in range(8):
                conv_dma(l, W2[l][j][:, :, :], w_ff2[l][j * 512:(j + 1) * 512, :].rearrange("(k p) n -> p k n", p=128))

        convert_A(0)

        NWB = 4
        wbuf = [sb("wb%d" % i, [128, 8, 512], BF16) for i in range(NWB)]
        wbufB = [Buf("wb%d" % i) for i in range(NWB)]
        wnext = [0]
        wg = sb("wg", [128, 8, 16], BF16); bwg = Buf("wg")
        bias16 = sb("bias16", [128, 16], F32)
        mhw = sb("mhw", [128, NH], F32)
        cw = sb("cw", [128, 4, 3], F32)
        g1 = sb("g1", [128, D], F32); b1 = sb("b1", [128, D], F32)
        g2 = sb("g2", [128, D], F32); b2 = sb("b2", [128, D], F32)
        bPar = Buf("params")
        xg = sb("xg", [128, TPG, D], F32); bxg = [Buf("xg%d" % t) for t in range(TPG)]
        xb16 = sb("xb16", [128, D], BF16); bxb16 = Buf("xb16")
        xT = sb("xT", [128, 8, G], BF16); bxT = [Buf("xT%d" % t) for t in range(TPG)]
        xh = sb("xh", [2, D], F32); xh16 = sb("xh16", [2, D], BF16); bxh = Buf("xh"); bxh16 = Buf("xh16")
        xTh = sb("xTh", [128, 8, 2], BF16); bxTh = Buf("xTh")
        hmT = sb("hmT", [128, 8, G], BF16); bhmT = [[Buf("hmT%d_%d" % (k, t)) for t in range(TPG)] for k in range(8)]
        x1T = sb("x1T", [128, 8, G], BF16); bx1T = [Buf("x1T%d" % t) for t in range(TPG)]
        h1T = sb("h1T", [128, 32, G], BF16); bh1T = [Buf("h1T%d" % f) for f in range(32)]
        wa = h1T[:].rearrange("p f t -> p (f t)")[:, 0:8 * 1040].rearrange("p (k n) -> p k n", k=8)
        bwa = [Buf("wa")] + bh1T[0:17]
        relu_t = [sb("relu%d" % i, [128, G], F32) for i in range(2)]; brelu = [Buf("relu%d" % i) for i in range(2)]
        zg = sb("zg", [128, TPG, 16], F32); bzg = Buf("zg")
        nlf = sb("nlf", [128, TPG, 8], F32); bnlf = Buf("nlf")
        eq = sb("eq", [128, TPG, 8], F32); ek = sb("ek", [128, TPG, 8], F32)
        ekk = sb("ekk", [128, TPG, 8], F32); eG = sb("eG", [128, TPG, 8], F32)
        tmp8 = sb("tmp8", [128, TPG, 8], F32); btmp8 = Buf("tmp8")
        bsc = Buf("scalars")
        NQ = 5
        NQB = 9
        qk = [sb("qk%d" % i, [128, 4, 128], BF16) for i in range(NQ)]; bqk = [Buf("qk%d" % i) for i in range(NQ)]
        k2 = [sb("k2%d" % i, [128, 128], BF16) for i in range(NQ)]; bk2 = [Buf("k2%d" % i) for i in range(NQ)]
        vx = [sb("vx%d" % i, [128, 129], BF16) for i in range(NQ)]; bvx = [Buf("vx%d" % i) for i in range(NQ)]
        qkT = [sb("qkT%d" % i, [128, 4, 128], BF16) for i in range(NQ)]; bqkT = [Buf("qkT%d" % i) for i in range(NQ)]
        st16 = [sb("st16%d" % i, [128, 2, 128], BF16) for i in range(NQ)]; bst16 = [Buf("st16%d" % i) for i in range(NQ)]
        sg = [sb("sg%d" % i, [128, 128], F32) for i in range(NQB)]; bsg = [Buf("sg%d" % i) for i in range(NQB)]
        hs = [sb("hs%d" % i, [128, 128], F32) for i in range(NQB)]; bhs = [Buf("hs%d" % i) for i in range(NQB)]
        gm = [sb("gm%d" % i, [128, 128], BF16) for i in range(NQB)]; bgm = [Buf("gm%d" % i) for i in range(NQB)]
        sm = [sb("sm%d" % i, [128, 16], F32) for i in range(NQB)]; bsm = [Buf("sm%d" % i) for i in range(NQB)]
        smL = [sb("smL%d" % i, [128, 4], F32) for i in range(TPG)]; bsmL = [Buf("smL%d" % i) for i in range(TPG)]
        CTf = sb("CTf", [128, NH, 129], F32); bCTf = [Buf("CTf%d" % h) for h in range(NH)]
        CTf16 = sb("CTf16", [128, NH, 129], BF16); bCTf16 = [Buf("CTf16_%d" % h) for h in range(NH)]
        CTb = sb("CTb", [128, NH, 129], F32); bCTb = Buf("CTb")
        CTb16 = [sb("CTb16_%d" % i, [128, NH * 129], BF16) for i in range(2)]; bCTb16 = [Buf("CTb16_%d" % i) for i in range(2)]
        sbg = sb("sbg", [128, TPG, NH * 129], BF16); bsbg = Buf("sbg")
        xa = [sb("xa%d" % i, [128, D], F32) for i in range(3)]; bxa = [Buf("xa%d" % i) for i in range(3)]
        xb16a = [sb("xb16a%d" % i, [128, D], BF16) for i in range(2)]; bxb16a = [Buf("xb16a%d" % i) for i in range(2)]
        xaT = [sb("xaT%d" % i, [128, 8, 128], BF16) for i in range(2)]; bxaT = [Buf("xaT%d" % i) for i in range(2)]
        k2a = [sb("k2a%d" % i, [128, 512], BF16) for i in range(2)]; bk2a = [Buf("k2a%d" % i) for i in range(2)]
        vxa = [sb("vxa%d" % i, [128, NH, 129], BF16) for i in range(2)]; bvxa = [Buf("vxa%d" % i) for i in range(2)]
        zga = sb("zga", [128, 2, 16], F32); bzga = [Buf("zga%d" % i) for i in range(2)]
        sca = sb("sca", [128, 2, 16], F32); bsca = [Buf("sca%d" % i) for i in range(2)]
        ccs = sb("ccs", [128, G + 2], F32); bccs = Buf("ccs")
        uu = sb("uu", [128, G + 2], F32); buu = Buf("uu")
        yy = sb("yy", [128, G], F32); byy = Buf("yy")
        hh = sb("hh", [128, 4], F32); bhh = Buf("hh")
        st6 = sb("st6", [128, 2, 6], F32); bst6 = Buf("st6")

        bank = [ps("bank%d" % i, [128, 512], F32) for i in range(8)]
        bbank = [[Buf("bank%d" % i)] for i in range(8)]
        Rtq = [bbank[3][0]] * 2
        Rst = [bbank[4][0]] * 2
        Rth = [bbank[7][0]] * 2
        R6g = [bbank[6][0]] * 2
        R6u = [bbank[6][0]] * 2

        def bank_bf16(i):
            return bank[i][:].bitcast(BF16)

        for i in range(NQ):
            S.op("pool", lambda e, i=i: e.memset(vx[i][:, 128:129], 1.0), writes=[bvx[i]])
        for i in range(2):
            S.op("pool", lambda e, i=i: e.memset(vxa[i][:, :, 128:129], 1.0), writes=[bvxa[i]])
        print("sbuf bytes remaining", nc.sbuf_bytes_remaining)

        def load_piece(l, src_ap, ncols_total=None, shape3=None):
            i = wnext[0]
            wnext[0] = (i + 1) % NWB
            dst = wbuf[i]
            if shape3 is None:
                S.dma("sp", lambda e: e.dma_start(out=dst[:, :, 0:src_ap.shape[2]], in_=src_ap), writes=[wbufB[i]])
                return dst, wbufB[i]
            v = dst[:].rearrange("p a b -> p (a b)").rearrange("p (a b) -> p a b", a=4)
            S.dma("sp", lambda e: e.dma_start(out=v, in_=src_ap), writes=[wbufB[i]])
            return v, wbufB[i]

        def load_params(l):
            S.dma("sp", lambda e: e.dma_start(out=bias16[:], in_=b_gate[l].partition_broadcast(128)), writes=[bPar])
            S.dma("sp", lambda e: e.dma_start(out=mhw[:], in_=mhw_in[l].rearrange("(h p) -> p h", p=128), allow_slow_non_contiguous=True), writes=[bPar])
            for j in range(4):
                S.dma("sp", lambda e, j=j: e.dma_start(out=cw[:, j, :], in_=convw_in[l][:, j * 128:(j + 1) * 128].rearrange("k p -> p k"), allow_slow_non_contiguous=True), writes=[bPar])
            for dst, src in ((g1, ln1g), (b1, ln1b), (g2, ln2g), (b2, ln2b)):
                S.dma("sp", lambda e, dst=dst, src=src: e.dma_start(out=dst[:], in_=src[l].partition_broadcast(128)), writes=[bPar])

        def gate_scalars(ntile, pgate, bpg, both_dirs):
            nt = ntile
            S.op("dve", lambda e: e.tensor_tensor(out=zg[:, 0:nt, :], in0=pgate, in1=bias16[:].unsqueeze(1).to_broadcast([128, nt, 16]), op=ALU.add),
                 reads=[bpg, bPar], writes=[bzg])
            zf = zg[:, 0:nt, :].rearrange("p t (d g h) -> p t d g h", d=2, g=2)[:, :, :, 1, :]
            S.op("act", lambda e: e.activation(out=nlf[:, 0:nt, :].rearrange("p t (d h) -> p t d h", d=2), in_=zf, func=AF.Exp, scale=-1.0), reads=[bzg], writes=[bnlf])
            S.op("act", lambda e: e.activation(out=nlf[:, 0:nt, :], in_=nlf[:, 0:nt, :], func=AF.Ln, bias=1.0, scale=1.0), reads=[bnlf], writes=[bnlf])

        def ln_rows(zap, bz, gt, bt_, smt, bsmt):
            S.op("dve", lambda e: e.bn_stats(out=st6[:, 0, :], in_=zap[:, 0:512]), reads=[bz], writes=[bst6])
            S.op("dve", lambda e: e.bn_stats(out=st6[:, 1, :], in_=zap[:, 512:1024]), reads=[bz], writes=[bst6])
            S.op("dve", lambda e: e.bn_aggr(out=smt[:, 0:2], in_=st6[:].rearrange("p a b -> p (a b)")), reads=[bst6], writes=[bsmt])
            S.op("dve", lambda e: e.tensor_scalar(out=smt[:, 2:3], in0=smt[:, 1:2], scalar1=EPS, scalar2=None, op0=ALU.add), reads=[bsmt], writes=[bsmt])
            S.op("act", lambda e: e.activation(out=smt[:, 2:3], in_=smt[:, 2:3], func=AF.Ln), reads=[bsmt], writes=[bsmt])
            S.op("act", lambda e: e.activation(out=smt[:, 3:4], in_=smt[:, 2:3], func=AF.Exp, scale=-0.5), reads=[bsmt], writes=[bsmt])
            S.op("dve", lambda e: e.scalar_tensor_tensor(out=zap, in0=zap, scalar=smt[:, 0:1], in1=gt[:], op0=ALU.subtract, op1=ALU.mult), reads=[bz, bsmt, bPar], writes=[bz])
            S.op("dve", lambda e: e.scalar_tensor_tensor(out=zap, in0=zap, scalar=smt[:, 3:4], in1=bt_[:], op0=ALU.mult, op1=ALU.add), reads=[bz, bsmt, bPar], writes=[bz])

        LNQ = float(np.log(DH ** -0.5))

        for l in range(depth):
            xin = x_in if l == 0 else xs[(l - 1) % 2]
            xout = y_out if l == depth - 1 else xs[l % 2]
            S.barrier("sp", "pool")
            if l == 0:
                convert_layer(0)
            if l + 1 < depth:
                convert_A(l + 1)
                convert_layer(l + 1)
            load_params(l)
            S.dma("sp", lambda e: e.dma_start(out=wa, in_=WA[l]), writes=[bwa])
            S.dma("sp", lambda e: e.dma_start(out=wg[:], in_=WG[l]), writes=[bwg])

            S.op("pool", lambda e: e.memset(CTb[:], 0.0), writes=[bCTb])
            order = list(reversed(range(NCH)))

            def A1cast(ci):
                c = order[ci]; p_ = ci % 2
                xa_t = xa[ci % 3]; bxa_t = bxa[ci % 3]
                S.dma("sp", lambda e: e.dma_start(out=xa_t[:], in_=xin[c * 128:(c + 1) * 128, :]), writes=[bxa_t])
                S.op("act", lambda e: e.copy(out=xb16a[p_][:], in_=xa_t[:]), reads=[bxa_t], writes=[bxb16a[p_]])

            def A1(ci):
                c = order[ci]; p_ = ci % 2
                pb = bank_bf16(p_)
                for k in range(8):
                    S.op("pe", lambda e, k=k: e.transpose(out=pb[:, k * 128:(k + 1) * 128], in_=xb16a[p_][:, k * 128:(k + 1) * 128], identity=ident[:]),
                         reads=[bxb16a[p_], bC], writes=[bbank[p_]], inc=(k == 7))
                S.op("dve", lambda e: e.tensor_copy(out=xaT[p_][:].rearrange("p k t -> p (k t)"), in_=pb), reads=[bbank[p_]], writes=[bxaT[p_]])

            def A2(ci, which):
                p_ = ci % 2
                parts = ((bank[2 + p_][:, :], bbank[2 + p_], 0, 512), (bank[4 + p_][:, :], bbank[4 + p_], 512, 512), (bank[6][:, p_ * 16:(p_ + 1) * 16], R6g[p_], 1024, 16))
                parts = parts[2:3] if which == "g" else parts[0:2]
                for (dst, bdst, c0, n) in parts:
                    for k in range(8):
                        S.op("pe", lambda e, k=k, dst=dst, c0=c0, n=n: e.matmul(dst, lhsT=xaT[p_][:, k, :], rhs=wa[:, k, c0:c0 + n], start=(k == 0), stop=(k == 7)),
                             reads=[bxaT[p_], bwa], writes=[bdst], inc=(k == 7))

            def A3pre(ci):
                p_ = ci % 2
                sc_ = sca[:, p_, :]
                S.op("dve", lambda e: e.tensor_tensor(out=zga[:, p_, :], in0=bank[6][:, p_ * 16:(p_ + 1) * 16], in1=bias16[:], op=ALU.add), reads=[R6g[p_], bPar], writes=[bzga[p_]])
                S.op("act", lambda e: e.activation(out=sc_[:, 0:4], in_=zga[:, p_, 12:16], func=AF.Exp, scale=-1.0), reads=[bzga[p_]], writes=[bsca[p_]])
                S.op("act", lambda e: e.activation(out=sc_[:, 0:4], in_=sc_[:, 0:4], func=AF.Ln, bias=1.0, scale=1.0), reads=[bsca[p_]], writes=[bsca[p_]])

            def A3a(ci):
                c = order[ci]; p_ = ci % 2
                sc_ = sca[:, p_, :]
                o6 = 32 + p_ * 8
                S.op("pe", lambda e: e.matmul(bank[6][:, o6:o6 + 4], lhsT=NSL[:], rhs=sc_[:, 0:4], start=True, stop=True), reads=[bsca[p_], bC], writes=[R6g[p_]], inc=False)
                S.op("pe", lambda e: e.matmul(bank[6][:, o6 + 4:o6 + 8], lhsT=ONES[:], rhs=sc_[:, 0:4], start=True, stop=True), reads=[bsca[p_], bC], writes=[R6g[p_]])
                S.op("dve", lambda e: e.tensor_tensor(out=sc_[:, 4:8], in0=bank[6][:, o6:o6 + 4], in1=zga[:, p_, 8:12], op=ALU.add), reads=[R6g[p_], bzga[p_], bsca[p_]], writes=[bsca[p_]])
                S.op("act", lambda e: e.activation(out=sc_[:, 8:12], in_=sc_[:, 4:8], func=AF.Exp), reads=[bsca[p_]], writes=[bsca[p_]])
                S.op("act", lambda e: e.activation(out=sc_[:, 12:16], in_=bank[6][:, o6 + 4:o6 + 8], func=AF.Exp, scale=-1.0), reads=[R6g[p_], bsca[p_]], writes=[bsca[p_]])
                for h in range(NH):
                    S.op("act", lambda e, h=h: e.activation(out=k2a[p_][:, h * 128:(h + 1) * 128], in_=bank[2 + p_][:, h * 128:(h + 1) * 128], func=AF.Copy, scale=sc_[:, 8 + h:9 + h]),
                         reads=[bbank[2 + p_], bsca[p_]], writes=[bk2a[p_]])
                S.op("dve", lambda e: e.tensor_copy(out=vxa[p_][:, :, 0:128], in_=bank[4 + p_][:].rearrange("p (h e) -> p h e", h=NH)), reads=[bbank[4 + p_]], writes=[bvxa[p_]])

            def A3b(ci):
                c = order[ci]; p_ = ci % 2
                sc_ = sca[:, p_, :]
                u3 = 64 + p_ * 129
                for h in range(NH):
                    dst, bdst = (bank[7][:, h * 129:(h + 1) * 129], bbank[7]) if h < 3 else (bank[6][:, u3:u3 + 129], R6u[p_])
                    S.op("pe", lambda e, h=h, dst=dst: e.matmul(dst, lhsT=k2a[p_][:, h * 128:(h + 1) * 128], rhs=vxa[p_][:, h, :], start=True, stop=True),
                         reads=[bk2a[p_], bvxa[p_]], writes=[bdst], inc=(h in (2, 3)))
                cb16 = CTb16[p_]; bcb16 = bCTb16[p_]
                S.op("pool", lambda e: e.tensor_copy(out=cb16[:], in_=CTb[:].rearrange("p h e -> p (h e)")), reads=[bCTb], writes=[bcb16])
                S.dma("pool", lambda e: e.dma_start(out=SBd[c], in_=cb16[:]), reads=[bcb16])
                for h in range(NH):
                    src, bsrc = (bank[7][:, h * 129:(h + 1) * 129], bbank[7]) if h < 3 else (bank[6][:, u3:u3 + 129], R6u[p_])
                    S.op("dve", lambda e, h=h, src=src: e.scalar_tensor_tensor(out=CTb[:, h, :], in0=CTb[:, h, :], scalar=sc_[:, 12 + h:13 + h], in1=src, op0=ALU.mult, op1=ALU.add),
                         reads=[bCTb, bsca[p_], bsrc], writes=[bCTb])
                if c == MIDC:
                    S.op("dve", lambda e: e.tensor_scalar(out=CTb[:], in0=CTb[:], scalar1=brk[:, 0:1], scalar2=None, op0=ALU.mult), reads=[bCTb, bC], writes=[bCTb])

            for s_ in range(NCH + 3):
                if s_ < NCH:
                    A1cast(s_)
                if 0 <= s_ - 1 < NCH:
                    A2(s_ - 1, "g")
                    A3pre(s_ - 1)
                if 0 <= s_ - 2 < NCH:
                    A3a(s_ - 2)
                if 0 <= s_ - 3 < NCH:
                    A3b(s_ - 3)
                if s_ < NCH:
                    A1(s_)
                if 0 <= s_ - 1 < NCH:
                    A2(s_ - 1, "kv")

            S.barrier("sp", "pool")
            S.op("pool", lambda e: e.memset(CTf[:], 0.0), writes=bCTf)
            S.op("pool", lambda e: e.memset(CTf16[:], 0.0), writes=bCTf16)

            def prologue(gn):
                t0n = gn * G
                for t in range(TPG):
                    xa_t = xa[t % 3]; bxa_t = bxa[t % 3]
                    S.dma("sp", lambda e, t=t, xa_t=xa_t: e.dma_start(out=xa_t[:], in_=xin[t0n + t * 128:t0n + (t + 1) * 128, :]), writes=[bxa_t])
                    xb_ = xb16a[t % 2]; bxb_ = bxb16a[t % 2]
                    S.op("act", lambda e, xa_t=xa_t, xb_=xb_: e.copy(out=xb_[:], in_=xa_t[:]), reads=[bxa_t], writes=[bxb_])
                    bi = 4 + (t % 2)
                    pb = bank_bf16(bi)
                    for k in range(8):
                        S.op("pe", lambda e, k=k, pb=pb, xb_=xb_: e.transpose(out=pb[:, k * 128:(k + 1) * 128], in_=xb_[:, k * 128:(k + 1) * 128], identity=ident[:]),
                             reads=[bxb_, bC], writes=[bbank[bi]], inc=(k == 7))
                    S.op("dve", lambda e, t=t, pb=pb: e.tensor_copy(out=xT[:, :, t * 128:(t + 1) * 128], in_=pb.rearrange("p (k t) -> p k t", k=8)), reads=[bbank[bi]], writes=[bxT[t]])
                rl = max(t0n - 1, 0); rr = min(t0n + G, T - 1)
                S.dma("sp", lambda e: e.dma_start(out=xh[0:1, :], in_=xin[rl:rl + 1, :]), writes=[bxh])
                S.dma("sp", lambda e: e.dma_start(out=xh[1:2, :], in_=xin[rr:rr + 1, :]), writes=[bxh])
                S.dma("sp", lambda e: e.dma_start(out=sbg[:], in_=SBd[gn * TPG:(gn + 1) * TPG].rearrange("t p f -> p t f")), writes=[bsbg])
                S.op("dve", lambda e: e.tensor_scalar(out=xh16[:], in0=xh[:], scalar1=hmask[:, gn:gn + 1], scalar2=None, op0=ALU.mult), reads=[bxh, bC], writes=[bxh16])
                pb = bank_bf16(6)
                for k in range(8):
                    S.op("pe", lambda e, k=k: e.transpose(out=pb[:, k * 2:(k + 1) * 2], in_=xh16[:, k * 128:(k + 1) * 128], identity=ident[0:2, 0:2]),
                         reads=[bxh16, bC], writes=[bbank[6]], inc=(k == 7))
                S.op("dve", lambda e: e.tensor_copy(out=xTh[:].rearrange("p k t -> p (k t)"), in_=pb[:, 0:16]), reads=[bbank[6]], writes=[bxTh])
                for t in range(TPG):
                    for k in range(8):
                        S.op("pe", lambda e, k=k, t=t: e.matmul(bank[7][:, t * 16:(t + 1) * 16], lhsT=xT[:, k, t * 128:(t + 1) * 128], rhs=wg[:, k, :], start=(k == 0), stop=(k == 7)),
                             reads=[bxT[t], bwg], writes=[bbank[7]], inc=(k == 7))
                gate_scalars(TPG, bank[7][:, 0:TPG * 16].rearrange("p (t c) -> p t c", t=TPG), bbank[7], True)
                for t in range(TPG):
                    o = 64 + t * 24
                    S.op("pe", lambda e, t=t, o=o: e.matmul(bank[7][:, o:o + 4], lhsT=UT[:], rhs=nlf[:, t, 0:4], start=True, stop=True), reads=[bnlf, bC], writes=[bbank[7]], inc=False)
                    S.op("pe", lambda e, t=t, o=o: e.matmul(bank[7][:, o + 4:o + 8], lhsT=LT[:], rhs=nlf[:, t, 4:8], start=True, stop=True), reads=[bnlf, bC], writes=[bbank[7]], inc=False)
                    S.op("pe", lambda e, t=t, o=o: e.matmul(bank[7][:, o + 8:o + 12], lhsT=NSG[:], rhs=nlf[:, t, 0:4], start=True, stop=True), reads=[bnlf, bC], writes=[bbank[7]], inc=False)
                    S.op("pe", lambda e, t=t, o=o: e.matmul(bank[7][:, o + 12:o + 16], lhsT=NSL[:], rhs=nlf[:, t, 4:8], start=True, stop=True), reads=[bnlf, bC], writes=[bbank[7]], inc=False)
                    S.op("pe", lambda e, t=t, o=o: e.matmul(bank[7][:, o + 16:o + 24], lhsT=ONES[:], rhs=nlf[:, t, :], start=True, stop=True), reads=[bnlf, bC], writes=[bbank[7]], inc=(t == TPG - 1))
                b3 = bank[7][:, 64:64 + TPG * 24].rearrange("p (t c) -> p t c", t=TPG)
                igv = zg[:].rearrange("p t (d g h) -> p t d g h", d=2, g=2)[:, :, :, 0, :]
                S.op("act", lambda e: e.activation(out=eq[:], in_=b3[:, :, 0:8], func=AF.Exp, scale=-1.0, bias=LNQ), reads=[bbank[7]], writes=[bsc])
                S.op("dve", lambda e: e.tensor_tensor(out=tmp8[:].rearrange("p t (d h) -> p t d h", d=2), in0=b3[:, :, 0:8].rearrange("p t (d h) -> p t d h", d=2), in1=igv, op=ALU.add), reads=[bbank[7], bzg], writes=[btmp8])
                S.op("act", lambda e: e.activation(out=ek[:], in_=tmp8[:], func=AF.Exp), reads=[btmp8], writes=[bsc])
                S.op("dve", lambda e: e.tensor_tensor(out=tmp8[:].rearrange("p t (d h) -> p t d h", d=2), in0=b3[:, :, 8:16].rearrange("p t (d h) -> p t d h", d=2), in1=igv, op=ALU.add), reads=[bbank[7], bzg, bsc], writes=[btmp8])
                S.op("act", lambda e: e.activation(out=ekk[:], in_=tmp8[:], func=AF.Exp), reads=[btmp8], writes=[bsc])
                S.op("act", lambda e: e.activation(out=eG[:], in_=b3[:, :, 16:24], func=AF.Exp, scale=-1.0), reads=[bbank[7]], writes=[bsc])

            prologue(0)
            pend_ln2 = []
            for g in range(NG):
                t0 = g * G
                def load_xg(t0=t0):
                    for t in range(TPG):
                        S.dma("sp", lambda e, t=t: e.dma_start(out=xg[:, t, :], in_=xin[t0 + t * 128:t0 + (t + 1) * 128, :]), writes=[bxg[t]])

                items = [(h, t) for h in range(NH) for t in range(TPG)]
                NI = len(items)
                wts = {}

                def st1(i):
                    h, t = items[i]
                    if t == 0:
                        wts[h] = load_piece(l, WH[l][h])
                    wt, bwt = wts[h]
                    q_ = i % NQ; qb_ = i % NQB; pbk = i % 3
                    for k in range(8):
                        S.op("pe", lambda e, k=k: e.matmul(bank[pbk][:, :], lhsT=xT[:, k, t * 128:(t + 1) * 128], rhs=wt[:, k, :], start=(k == 0), stop=(k == 7)),
                             reads=[bxT[t], bwt], writes=[bbank[pbk]], inc=(k == 7))
                    pj = bank[pbk]
                    S.op("act", lambda e: e.activation(out=qk[q_][:, 0, :], in_=pj[:, 0:128], func=AF.Copy, scale=eq[:, t, h:h + 1]), reads=[bbank[pbk], bsc], writes=[bqk[q_]])
                    S.op("act", lambda e: e.activation(out=qk[q_][:, 1, :], in_=pj[:, 0:128], func=AF.Copy, scale=eq[:, t, 4 + h:5 + h]), reads=[bbank[pbk], bsc], writes=[bqk[q_]])
                    S.op("act", lambda e: e.activation(out=qk[q_][:, 2, :], in_=pj[:, 128:256], func=AF.Copy, scale=ek[:, t, h:h + 1]), reads=[bbank[pbk], bsc], writes=[bqk[q_]])
                    S.op("dve", lambda e: e.tensor_scalar(out=qk[q_][:, 3, :], in0=pj[:, 128:256], scalar1=ek[:, t, 4 + h:5 + h], scalar2=None, op0=ALU.mult), reads=[bbank[pbk], bsc], writes=[bqk[q_]])
                    S.op("dve", lambda e: e.tensor_scalar(out=k2[q_][:], in0=pj[:, 128:256], scalar1=ekk[:, t, h:h + 1], scalar2=None, op0=ALU.mult), reads=[bbank[pbk], bsc], writes=[bk2[q_]])
                    S.op("act", lambda e: e.copy(out=vx[q_][:, 0:128], in_=pj[:, 256:384]), reads=[bbank[pbk]], writes=[bvx[q_]])
                    S.op("act", lambda e: e.activation(out=sg[qb_][:], in_=pj[:, 384:512], func=AF.Exp, scale=-1.0), reads=[bbank[pbk]], writes=[bsg[qb_]])
                    S.op("act", lambda e: e.activation(out=sg[qb_][:], in_=sg[qb_][:], func=AF.Ln, bias=1.0, scale=1.0), reads=[bsg[qb_]], writes=[bsg[qb_]])
                    S.op("act", lambda e: e.activation(out=sg[qb_][:], in_=sg[qb_][:], func=AF.Exp, scale=-1.0), reads=[bsg[qb_]], writes=[bsg[qb_]])

                def st2(i):
                    q_ = i % NQ; r_ = i % 2
                    pT = bank_bf16(3)[:, r_ * 512:(r_ + 1) * 512]
                    for a_ in range(4):
                        S.op("pe", lambda e, a_=a_: e.transpose(out=pT[:, a_ * 128:(a_ + 1) * 128], in_=qk[q_][:, a_, :], identity=ident[:]),
                             reads=[bqk[q_], bC], writes=[Rtq[r_]], inc=(a_ == 3))
                    S.op("act", lambda e: e.copy(out=qkT[q_][:].rearrange("p a t -> p (a t)"), in_=pT), reads=[Rtq[r_]], writes=[bqkT[q_]])

                def st3(i):
                    q_ = i % NQ; r_ = i % 2
                    pS = bank[4][:, r_ * 256:(r_ + 1) * 256]
                    for d_ in range(2):
                        S.op("pe", lambda e, d_=d_: e.matmul(pS[:, d_ * 128:(d_ + 1) * 128], lhsT=qkT[q_][:, 2 + d_, :], rhs=qkT[q_][:, d_, :], start=True, stop=True),
                             reads=[bqkT[q_]], writes=[Rst[r_]], inc=(d_ == 1))
                    S.op("dve", lambda e: e.tensor_tensor(out=st16[q_][:], in0=pS.rearrange("p (d t) -> p d t", d=2), in1=mask2[:], op=ALU.mult), reads=[Rst[r_], bC], writes=[bst16[q_]])

                def st4(i):
                    h, t = items[i]
                    c = g * TPG + t
                    q_ = i % NQ; qb_ = i % NQB; ob = 5 + (i % 2)
                    if c == MIDC and h == 0:
                        S.op("dve", lambda e: e.tensor_scalar(out=CTf[:], in0=CTf[:], scalar1=brk[:, 0:1], scalar2=None, op0=ALU.mult), reads=bCTf + [bC], writes=bCTf)
                        S.op("act", lambda e: e.copy(out=CTf16[:], in_=CTf[:]), reads=bCTf, writes=bCTf16)
                    po = bank[ob]
                    S.op("pe", lambda e: e.matmul(po[:, 0:129], lhsT=st16[q_][:, 0, :], rhs=vx[q_][:], start=True, stop=False), reads=[bst16[q_], bvx[q_]], writes=[bbank[ob]], inc=False)
                    S.op("pe", lambda e: e.matmul(po[:, 0:129], lhsT=qkT[q_][:, 0, :], rhs=CTf16[:, h, :], start=False, stop=True), reads=[bqkT[q_], bCTf16[h]], writes=[bbank[ob]], inc=False)
                    S.op("pe", lambda e: e.matmul(po[:, 129:258], lhsT=st16[q_][:, 1, :], rhs=vx[q_][:], start=True, stop=False), reads=[bst16[q_], bvx[q_]], writes=[bbank[ob]], inc=False)
                    S.op("pe", lambda e: e.matmul(po[:, 129:258], lhsT=qkT[q_][:, 1, :], rhs=sbg[:, t, h * 129:(h + 1) * 129], start=False, stop=True), reads=[bqkT[q_], bsbg], writes=[bbank[ob]], inc=False)
                    S.op("pe", lambda e: e.matmul(po[:, 258:387], lhsT=k2[q_][:], rhs=vx[q_][:], start=True, stop=True), reads=[bk2[q_], bvx[q_]], writes=[bbank[ob]])
                    S.op("dve", lambda e: e.scalar_tensor_tensor(out=CTf[:, h, :], in0=CTf[:, h, :], scalar=eG[:, t, h:h + 1], in1=po[:, 258:387], op0=ALU.mult, op1=ALU.add),
                         reads=[bCTf[h], bsc, bbank[ob]], writes=[bCTf[h]])
                    smq = sm[qb_]; bsmq = bsm[qb_]
                    dens = po[:, 0:258].rearrange("p (d e) -> p d e", d=2)[:, :, 128]
                    S.op("dve", lambda e: e.tensor_scalar(out=smq[:, 0:2], in0=dens, scalar1=-1.0, scalar2=1.0, op0=ALU.mult, op1=ALU.max), reads=[bbank[ob]], writes=[bsmq])
                    S.op("dve", lambda e: e.scalar_tensor_tensor(out=smq[:, 0:2], in0=dens, scalar=1.0, in1=smq[:, 0:2], op0=ALU.max, op1=ALU.max), reads=[bbank[ob], bsmq], writes=[bsmq])
                    S.op("dve", lambda e: e.reciprocal(out=smq[:, 2:4], in_=smq[:, 0:2]), reads=[bsmq], writes=[bsmq])
                    S.op("dve", lambda e: e.tensor_scalar(out=hs[qb_][:], in0=po[:, 0:128], scalar1=smq[:, 2:3], scalar2=None, op0=ALU.mult), reads=[bbank[ob], bsmq], writes=[bhs[qb_]])
                    S.op("dve", lambda e: e.scalar_tensor_tensor(out=hs[qb_][:], in0=po[:, 129:257], scalar=smq[:, 3:4], in1=hs[qb_][:], op0=ALU.mult, op1=ALU.add),
                         reads=[bbank[ob], bsmq, bhs[qb_]], writes=[bhs[qb_]])
                    S.op("dve", lambda e: e.bn_stats(out=smq[:, 4:10], in_=hs[qb_][:]), reads=[bhs[qb_], bsmq], writes=[bsmq])
                    S.op("dve", lambda e: e.bn_aggr(out=smq[:, 10:12], in_=smq[:, 4:10]), reads=[bsmq], writes=[bsmq])
                    S.op("dve", lambda e: e.tensor_scalar(out=smq[:, 12:13], in0=smq[:, 11:12], scalar1=EPS, scalar2=None, op0=ALU.add), reads=[bsmq], writes=[bsmq])
                    S.op("act", lambda e: e.copy(out=CTf16[:, h, :], in_=CTf[:, h, :]), reads=[bCTf[h]], writes=[bCTf16[h]])

                def st4b_act(i):
                    qb_ = i % NQB
                    smq = sm[qb_]; bsmq = bsm[qb_]
                    S.op("act", lambda e: e.activation(out=smq[:, 12:13], in_=smq[:, 12:13], func=AF.Ln), reads=[bsmq], writes=[bsmq])
                    S.op("act", lambda e: e.activation(out=smq[:, 13:14], in_=smq[:, 12:13], func=AF.Exp, scale=-0.5), reads=[bsmq], writes=[bsmq])

                def st4b(i):
                    qb_ = i % NQB
                    smq = sm[qb_]; bsmq = bsm[qb_]
                    S.op("dve", lambda e: e.tensor_scalar(out=hs[qb_][:], in0=hs[qb_][:], scalar1=smq[:, 10:11], scalar2=smq[:, 13:14], op0=ALU.subtract, op1=ALU.mult), reads=[bhs[qb_], bsmq], writes=[bhs[qb_]])
                    S.op("dve", lambda e: e.tensor_tensor(out=gm[qb_][:], in0=hs[qb_][:], in1=sg[qb_][:], op=ALU.mult), reads=[bhs[qb_], bsg[qb_]], writes=[bgm[qb_]])

                def st5(i):
                    h, t = items[i]
                    qb_ = i % NQB; r_ = i % 2
                    pT2 = bank_bf16(7)[:, r_ * 128:(r_ + 1) * 128]
                    S.op("pe", lambda e: e.transpose(out=pT2, in_=gm[qb_][:], identity=ident[:]), reads=[bgm[qb_], bC], writes=[Rth[r_]])
                    S.op("act", lambda e: e.activation(out=hmT[:, h, t * 128:(t + 1) * 128], in_=pT2, func=AF.Copy, scale=mhw[:, h:h + 1]), reads=[Rth[r_], bPar], writes=[bhmT[h][t]])

                conv_w = {}

                def conv_mm(j):
                    wt, bwt = load_piece(l, WC[l][j])
                    conv_w[j] = (wt, bwt)
                    bs_ = (0, 1, 2) if j % 2 == 0 else (3, 4, 5)
                    for i in range(3):
                        for k in range(8):
                            S.op("pe", lambda e, k=k, i=i: e.matmul(bank[bs_[i]][:, :], lhsT=wt[:, k, i * 128:(i + 1) * 128], rhs=xT[:, k, :], start=(k == 0), stop=(k == 7)),
                                 reads=bxT + [bwt], writes=[bbank[bs_[i]]], inc=(k == 7))
                    ho = 256 + (j % 2) * 4
                    for i in (1, 2):
                        for k in range(8):
                            S.op("pe", lambda e, k=k, i=i: e.matmul(bank[7][:, ho + (i - 1) * 2:ho + (i - 1) * 2 + 2], lhsT=wt[:, k, i * 128:(i + 1) * 128], rhs=xTh[:, k, :], start=(k == 0), stop=(k == 7)),
                                 reads=[bxTh, bwt], writes=[bbank[7]], inc=(k == 7 and i == 2))

                def conv_ev(j):
                    bs_ = (0, 1, 2) if j % 2 == 0 else (3, 4, 5)
                    ho = 256 + (j % 2) * 4
                    S.op("act", lambda e: e.copy(out=ccs[:, 1:G + 1], in_=bank[bs_[1]][:, :]), reads=[bbank[bs_[1]]], writes=[bccs])
                    S.op("act", lambda e: e.copy(out=hh[:], in_=bank[7][:, ho:ho + 4]), reads=[bbank[7]], writes=[bhh])
                    S.op("dve", lambda e: e.tensor_tensor(out=uu[:, 1:G + 1], in0=ccs[:, 1:G + 1], in1=bank[bs_[2]][:, :], op=ALU.mult), reads=[bccs, bbank[bs_[2]]], writes=[buu])
                    S.op("dve", lambda e: e.tensor_tensor(out=uu[:, 0:1], in0=hh[:, 0:1], in1=hh[:, 2:3], op=ALU.mult), reads=[bhh, buu], writes=[buu])
                    S.op("dve", lambda e: e.tensor_tensor(out=uu[:, G + 1:G + 2], in0=hh[:, 1:2], in1=hh[:, 3:4], op=ALU.mult), reads=[bhh, buu], writes=[buu])
                    S.op("dve", lambda e: e.tensor_scalar(out=yy[:], in0=uu[:, 0:G], scalar1=cw[:, j, 0:1], scalar2=None, op0=ALU.mult), reads=[buu, bPar], writes=[byy])
                    S.op("dve", lambda e: e.scalar_tensor_tensor(out=yy[:], in0=uu[:, 1:G + 1], scalar=cw[:, j, 1:2], in1=yy[:], op0=ALU.mult, op1=ALU.add), reads=[buu, bPar, byy], writes=[byy])
                    S.op("dve", lambda e: e.scalar_tensor_tensor(out=yy[:], in0=uu[:, 2:G + 2], scalar=cw[:, j, 2:3], in1=yy[:], op0=ALU.mult, op1=ALU.add), reads=[buu, bPar, byy], writes=[byy])
                    S.op("dve", lambda e: e.tensor_tensor(out=hmT[:, 4 + j, :], in0=yy[:], in1=bank[bs_[0]][:, :], op=ALU.mult), reads=[byy, bbank[bs_[0]]], writes=bhmT[4 + j])

                conv_sched = {1: [("mm", 0)], 3: [("mm", 1), ("ev", 0)], 4: [("mm", 2)], 5: [("ev", 1)], 6: [("mm", 3), ("ev", 2)], 8: [("ev", 3)]}

                for s_ in range(NI + 9):
                    if 0 <= s_ - 4 < NI:
                        st4(s_ - 4)
                    if 0 <= s_ - 5 < NI:
                        st4b_act(s_ - 5)
                    if 0 <= s_ - 8 < NI:
                        st5(s_ - 8)
                    if 0 <= s_ - 2 < NI:
                        st2(s_ - 2)
                    if 0 <= s_ - 3 < NI:
                        st3(s_ - 3)
                    if 0 <= s_ - 5 < NI:
                        st4b(s_ - 5)
                    if s_ < NI:
                        st1(s_)
                    if s_ in (2, 4, 6, 8) and pend_ln2:
                        pend_ln2.pop(0)()
                    if s_ == 9:
                        load_xg()
                    for kind, j in conv_sched.get(s_ - NI, ()):
                        (conv_mm if kind == "mm" else conv_ev)(j)

                wo = [load_piece(l, WO[l][j]) for j in range(2)]

                def outproj(t):
                    for j in range(2):
                        bi = (t % 2) * 2 + j
                        for k in range(8):
                            S.op("pe", lambda e, k=k, j=j, bi=bi: e.matmul(bank[bi][:, :], lhsT=hmT[:, k, t * 128:(t + 1) * 128], rhs=wo[j][0][:, k, :], start=(k == 0), stop=(k == 7)),
                                 reads=[bhmT[k][t], wo[j][1]], writes=[bbank[bi]], inc=(k == 7))
                        S.op("dve", lambda e, j=j, bi=bi: e.scalar_tensor_tensor(out=xg[:, t, j * 512:(j + 1) * 512], in0=xg[:, t, j * 512:(j + 1) * 512], scalar=ALPHA, in1=bank[bi][:, :], op0=ALU.mult, op1=ALU.add),
                             reads=[bxg[t], bbank[bi]], writes=[bxg[t]])
                    ln_rows(xg[:, t, :], bxg[t], g1, b1, smL[t], bsmL[t])
                    S.op("act", lambda e: e.copy(out=xb16a[t % 2][:], in_=xg[:, t, :]), reads=[bxg[t]], writes=[bxb16a[t % 2]])

                def x1_transpose(t):
                    pbi = 4 + (t % 2)
                    pb = bank_bf16(pbi)
                    for k in range(8):
                        S.op("pe", lambda e, k=k, pb=pb: e.transpose(out=pb[:, k * 128:(k + 1) * 128], in_=xb16a[t % 2][:, k * 128:(k + 1) * 128], identity=ident[:]),
                             reads=[bxb16a[t % 2], bC], writes=[bbank[pbi]], inc=(k == 7))
                    S.op("act", lambda e, pb=pb: e.copy(out=x1T[:, :, t * 128:(t + 1) * 128], in_=pb.rearrange("p (k t) -> p k t", k=8)), reads=[bbank[pbi]], writes=[bx1T[t]])

                for t in range(TPG + 1):
                    if t < TPG:
                        outproj(t)
                    if t >= 1:
                        x1_transpose(t - 1)

                for p in range(8):
                    if p == 4 and g + 1 < NG:
                        prologue(g + 1)
                    wt, bwt = load_piece(l, W1[l][p])
                    for fc in range(4):
                        f = 4 * p + fc
                        bi = f % 4
                        for k in range(8):
                            S.op("pe", lambda e, k=k, fc=fc, bi=bi: e.matmul(bank[bi][:, :], lhsT=wt[:, k, fc * 128:(fc + 1) * 128], rhs=x1T[:, k, :], start=(k == 0), stop=(k == 7)),
                                 reads=bx1T + [bwt], writes=[bbank[bi]], inc=(k == 7))
                        r_ = f % 2
                        S.op("act", lambda e, bi=bi, r_=r_: e.activation(out=relu_t[r_][:], in_=bank[bi][:, :], func=AF.Relu), reads=[bbank[bi]], writes=[brelu[r_]])
                        eng2 = "pool" if (f % 2) else "dve"
                        S.op(eng2, lambda e, f=f, r_=r_: e.tensor_tensor(out=h1T[:, f, :], in0=relu_t[r_][:], in1=relu_t[r_][:], op=ALU.mult), reads=[brelu[r_]], writes=[bh1T[f]])

                for p in range(8):
                    wt, bwt = load_piece(l, W2[l][p], shape3=True)
                    for t in range(TPG):
                        for j in range(2):
                            bi = t * 2 + j
                            for fc in range(4):
                                f = 4 * p + fc
                                S.op("pe", lambda e, t=t, j=j, bi=bi, fc=fc, f=f, p=p: e.matmul(bank[bi][:, :], lhsT=h1T[:, f, t * 128:(t + 1) * 128], rhs=wt[:, fc, j * 512:(j + 1) * 512],
                                                                                             start=(p == 0 and fc == 0), stop=(p == 7 and fc == 3)),
                                     reads=[bh1T[f], bwt], writes=[bbank[bi]], inc=(fc == 3))
                for t in range(TPG):
                    for j in range(2):
                        bi = t * 2 + j
                        S.op("dve", lambda e, t=t, j=j, bi=bi: e.scalar_tensor_tensor(out=xg[:, t, j * 512:(j + 1) * 512], in0=xg[:, t, j * 512:(j + 1) * 512], scalar=ALPHA, in1=bank[bi][:, :], op0=ALU.mult, op1=ALU.add),
                             reads=[bxg[t], bbank[bi]], writes=[bxg[t]])

                def ln2_rest(t, t0=t0):
                    ln_rows(xg[:, t, :], bxg[t], g2, b2, smL[t], bsmL[t])
                    S.dma("pool", lambda e: e.dma_start(out=xout[t0 + t * 128:t0 + (t + 1) * 128, :], in_=xg[:, t, :]), reads=[bxg[t]])

                for t in range(TPG):
                    pend_ln2.append(lambda t=t, f=ln2_rest: f(t))
                if g == NG - 1:
                    while pend_ln2:
                        pend_ln2.pop(0)()
        S.finish()
    return nc, S


def _run(inputs, T, depth, seqs):
    nc, S = build_program(T, depth)
    NG = T // G
    in_maps = []
    for ci in range(8):
        x, brkv = seqs[ci]
        hm = np.ones((2, NG), np.float32)
        hm[0, 0] = 0.0
        hm[1, NG - 1] = 0.0
        hm[0, NG // 2] = brkv
        hm[1, NG // 2 - 1] = brkv
        m = {"x": np.ascontiguousarray(x, dtype=np.float32),
             "brk": np.full((128, 1), brkv, np.float32),
             "hmask": hm,
             "w_in": np.ascontiguousarray(inputs["w_in"][:depth]),
             "b_gate": np.ascontiguousarray(inputs["b_gate"][:depth]).reshape(depth, 16),
             "mh_norm_w": np.ascontiguousarray(inputs["mh_norm_w"][:depth]),
             "conv_w": np.ascontiguousarray(inputs["conv_w"][:depth]),
             "w_out": np.ascontiguousarray(inputs["w_out"][:depth]),
             "ln1_g": np.ascontiguousarray(inputs["ln1_g"][:depth]),
             "ln1_b": np.ascontiguousarray(inputs["ln1_b"][:depth]),
             "w_ff1": np.ascontiguousarray(inputs["w_ff1"][:depth]),
             "w_ff2": np.ascontiguousarray(inputs["w_ff2"][:depth]),
             "ln2_g": np.ascontiguousarray(inputs["ln2_g"][:depth]),
             "ln2_b": np.ascontiguousarray(inputs["ln2_b"][:depth])}
        in_maps.append(m)
    res = run_bass_kernel_spmd(nc, in_maps, core_ids=list(range(8)))
    return [r["y"] for r in res.results]


def kernel(x_prompt, x_sample, w_in, b_gate, mh_norm_w, conv_w, w_out,
           ln1_g, ln1_b, w_ff1, w_ff2, ln2_g, ln2_b):
    inputs = dict(w_in=np.asarray(w_in, np.float32), b_gate=np.asarray(b_gate, np.float32), mh_norm_w=np.asarray(mh_norm_w, np.float32),
                  conv_w=np.asarray(conv_w, np.float32), w_out=np.asarray(w_out, np.float32), ln1_g=np.asarray(ln1_g, np.float32),
                  ln1_b=np.asarray(ln1_b, np.float32), w_ff1=np.asarray(w_ff1, np.float32), w_ff2=np.asarray(w_ff2, np.float32),
                  ln2_g=np.asarray(ln2_g, np.float32), ln2_b=np.asarray(ln2_b, np.float32))
    xp = np.asarray(x_prompt, np.float32)
    xsm = np.asarray(x_sample, np.float32)
    T = xp.shape[1]
    assert xsm.shape[0] * xsm.shape[1] == T
    zero = np.zeros((T, D), np.float32)
    seqs = [(xp[0], 1.0), (xsm.reshape(T, D), 0.0)] + [(zero, 0.0)] * 6
    ys = _run(inputs, T, DEPTH, seqs)
    y_prompt = ys[0].reshape(xp.shape).astype(np.float32)
    y_sample = ys[1].reshape(xsm.shape).astype(np.float32)
    return (y_prompt, y_sample)
```
